# Optimizing a Trainium2 kernel written in Bass

```python
import math
import jax
import jax.numpy as jnp
from jax import lax
import numpy as np

D_MODEL = 4096
BATCH = 1
SEQ = 16384
DEPTH = 4

HEAD_DIM = 128
N_HEADS = D_MODEL // HEAD_DIM
N_HEADS_A = N_HEADS // 2
N_HEADS_B = N_HEADS - N_HEADS_A
D_A = N_HEADS_A * HEAD_DIM
D_B = N_HEADS_B * HEAD_DIM
D_MIX = D_A + D_B
DILATED_PATTERNS = ((128, 1), (512, 4), (2048, 16))
Q_BLOCK = 128
NSA_KV_HEADS = 2
NSA_GROUP = N_HEADS_B // NSA_KV_HEADS
D_KV_B = NSA_KV_HEADS * HEAD_DIM
CMP_BLOCK = 32
CMP_STRIDE = 16
CMP_HIDDEN = 2 * HEAD_DIM
SLC_BLOCK = 64
SLC_TOP = 16
SLC_LOCAL = 2
WIN_B = 512
NORM_EPS = 1e-6

kernel_name = 'hybrid_dilated_nsa_block'


def _in_widths():
    return (D_A, D_A, D_A, D_A, D_B, D_KV_B, D_KV_B, D_KV_B, D_KV_B, D_KV_B, D_KV_B, D_B,
            3 * N_HEADS_B)


def _split_points():
    w = _in_widths()
    return [sum(w[:i + 1]) for i in range(len(w) - 1)]


def _alibi_slopes(n):
    return jnp.exp2(-8.0 * jnp.arange(1, n + 1, dtype=jnp.float32) / n)


def _rmsnorm(x, g):
    xf = x.astype(jnp.float32)
    y = xf * lax.rsqrt(jnp.mean(xf * xf, axis=-1, keepdims=True) + NORM_EPS)
    return (y * g.astype(jnp.float32)).astype(x.dtype)


def _masked_softmax(scores, valid):
    s = jnp.where(valid, scores, -jnp.inf)
    m = jnp.max(s, axis=-1, keepdims=True)
    m = jnp.where(jnp.isfinite(m), m, 0.0)
    e = jnp.exp(s - m)
    den = jnp.sum(e, axis=-1, keepdims=True)
    den = jnp.where(den > 0, den, 1.0)
    return e / den, (m + jnp.log(den))[..., 0]


def _dilated_branch(q, k, v, window, dilation, slopes):
    B, S, H, hd = q.shape
    L = S // dilation
    W = window // dilation
    nb = -(-L // W)
    Lp = nb * W

    def sub(t, front):
        t = t.reshape(B, L, dilation, H, hd).transpose(0, 2, 1, 3, 4)
        return jnp.pad(t, ((0, 0), (0, 0), (front, Lp - L), (0, 0), (0, 0)))

    qb = sub(q, 0).reshape(B, dilation, nb, W, H, hd)

    def band(t):
        tp = sub(t, W).reshape(B, dilation, nb + 1, W, H, hd)
        return jnp.concatenate([tp[:, :, :-1], tp[:, :, 1:]], axis=3)

    kb, vb = band(k), band(v)
    scores = jnp.einsum('brnqhd,brnkhd->brnhqk', qb, kb)
    dist = W + jnp.arange(W)[:, None] - jnp.arange(2 * W)[None, :]
    key_idx = (jnp.arange(nb)[:, None] - 1) * W + jnp.arange(2 * W)[None, :]
    valid = ((dist >= 0) & (dist <= W))[None] & (key_idx >= 0)[:, None, :]
    scores = scores - slopes[:, None, None] * (dilation * dist).astype(jnp.float32)
    p, lse = _masked_softmax(scores, valid[None, None, :, None])
    o = jnp.einsum('brnhqk,brnkhd->brnqhd', p, vb)
    o = o.reshape(B, dilation, Lp, H, hd)[:, :, :L].transpose(0, 2, 1, 3, 4).reshape(B, S, H, hd)
    lse = lse.transpose(0, 1, 2, 4, 3).reshape(B, dilation, Lp, H)[:, :, :L]
    lse = lse.transpose(0, 2, 1, 3).reshape(B, S, H)
    return o, lse


def _dilated_mixture(q, k, v, slopes):
    outs, lses = [], []
    for window, dilation in DILATED_PATTERNS:
        o, l = _dilated_branch(q, k, v, window, dilation, slopes)
        outs.append(o)
        lses.append(l)
    w = jax.nn.softmax(jnp.stack(lses, axis=0), axis=0)
    return jnp.einsum('ibsh,ibshd->bshd', w, jnp.stack(outs, axis=0))


def _compress(t, pe, w1, w2):
    B, S, G, hd = t.shape
    n_c = (S - CMP_BLOCK) // CMP_STRIDE + 1
    idx = jnp.arange(n_c)[:, None] * CMP_STRIDE + jnp.arange(CMP_BLOCK)[None, :]
    blocks = t[:, idx] + pe.astype(jnp.float32)[None, None, :, None, :]
    flat = blocks.transpose(0, 1, 3, 2, 4).reshape(B, n_c, G, CMP_BLOCK * hd)
    return jax.nn.gelu(flat @ w1) @ w2


def _nsa(q, kc, vc, ks, vs, kw, vw, gates, slopes, k_pe, k_w1, k_w2, v_pe, v_w1, v_w2):
    B, S, H, hd = q.shape
    G, R = NSA_KV_HEADS, NSA_GROUP
    kcmp = _compress(kc, k_pe, k_w1, k_w2)
    vcmp = _compress(vc, v_pe, v_w1, v_w2)
    n_c = kcmp.shape[1]
    c_start = jnp.arange(n_c) * CMP_STRIDE
    c_end = c_start + CMP_BLOCK - 1
    n_s = S // SLC_BLOCK
    top = min(SLC_TOP, n_s)
    blk = jnp.arange(n_s)
    j_start = blk * SLC_BLOCK
    overlap = ((c_start[:, None] < j_start[None, :] + SLC_BLOCK)
               & (c_end[:, None] >= j_start[None, :])).astype(jnp.float32)
    ks_b = ks.reshape(B, n_s, SLC_BLOCK, G, hd).transpose(0, 3, 1, 2, 4)
    vs_b = vs.reshape(B, n_s, SLC_BLOCK, G, hd).transpose(0, 3, 1, 2, 4)
    kw_p = jnp.pad(kw, ((0, 0), (WIN_B, 0), (0, 0), (0, 0)))
    vw_p = jnp.pad(vw, ((0, 0), (WIN_B, 0), (0, 0), (0, 0)))
    qg = q.reshape(B, S, G, R, hd) * (hd ** -0.5)
    gg = gates.reshape(B, S, G, R, 3)
    sl = slopes.reshape(G, R)
    b_ix = jnp.arange(B)[:, None, None, None]
    g_ix = jnp.arange(G)[None, :, None, None]

    def one_block(n):
        t0 = n * Q_BLOCK
        t = t0 + jnp.arange(Q_BLOCK)
        qb = lax.dynamic_slice_in_dim(qg, t0, Q_BLOCK, axis=1)
        dist_c = t[:, None] - c_end[None, :]
        sc = (jnp.einsum('bqgrd,bcgd->bgrqc', qb, kcmp)
              - sl[:, :, None, None] * dist_c.astype(jnp.float32))
        p_c, _ = _masked_softmax(sc, dist_c >= 0)
        o_c = jnp.einsum('bgrqc,bcgd->bqgrd', p_c, vcmp)
        imp = jnp.einsum('bgrqc,cj->bgqj', p_c, overlap)
        cur = (t // SLC_BLOCK)[:, None]
        causal_blk = blk[None, :] <= cur
        forced = (blk[None, :] == 0) | (causal_blk & ((cur - blk[None, :]) < SLC_LOCAL))
        imp = jnp.where(forced, jnp.inf, jnp.where(causal_blk, imp, -jnp.inf))
        vals, idx = lax.top_k(imp, top)
        k_sel = ks_b[b_ix, g_ix, idx]
        v_sel = vs_b[b_ix, g_ix, idx]
        pos_s = idx[..., None] * SLC_BLOCK + jnp.arange(SLC_BLOCK)
        dist_s = (t[:, None, None] - pos_s)[:, :, None]
        valid_s = (vals > -jnp.inf)[:, :, None, :, :, None] & (dist_s >= 0)
        ss = (jnp.einsum('bqgrd,bgqjsd->bgrqjs', qb, k_sel)
              - sl[None, :, :, None, None, None] * dist_s.astype(jnp.float32))
        p_s, _ = _masked_softmax(ss.reshape(B, G, R, Q_BLOCK, top * SLC_BLOCK),
                                 valid_s.reshape(B, G, 1, Q_BLOCK, top * SLC_BLOCK))
        o_s = jnp.einsum('bgrqjs,bgqjsd->bqgrd', p_s.reshape(ss.shape), v_sel)
        kwb = lax.dynamic_slice_in_dim(kw_p, t0, Q_BLOCK + WIN_B, axis=1)
        vwb = lax.dynamic_slice_in_dim(vw_p, t0, Q_BLOCK + WIN_B, axis=1)
        pos_w = t0 - WIN_B + jnp.arange(Q_BLOCK + WIN_B)
        dist_w = t[:, None] - pos_w[None, :]
        valid_w = (dist_w >= 0) & (dist_w < WIN_B) & (pos_w[None, :] >= 0)
        sw = (jnp.einsum('bqgrd,bkgd->bgrqk', qb, kwb)
              - sl[:, :, None, None] * dist_w.astype(jnp.float32))
        p_w, _ = _masked_softmax(sw, valid_w)
        o_w = jnp.einsum('bgrqk,bkgd->bqgrd', p_w, vwb)
        gb = lax.dynamic_slice_in_dim(gg, t0, Q_BLOCK, axis=1)
        return gb[..., 0:1] * o_c + gb[..., 1:2] * o_s + gb[..., 2:3] * o_w

    out = lax.map(one_block, jnp.arange(S // Q_BLOCK))
    return out.transpose(1, 0, 2, 3, 4, 5).reshape(B, S, H * hd)


def setup_inputs(seed: int = 0) -> dict:
    key = jax.random.key(seed)
    ks = jax.random.split(key, 14)
    d_in = sum(_in_widths())
    f = CMP_BLOCK * HEAD_DIM
    nrm = jax.random.normal
    return {
        'x': nrm(ks[0], (BATCH, SEQ, D_MODEL), jnp.float32),
        'norm_g': 1.0 + 0.02 * nrm(ks[1], (DEPTH, D_MODEL), jnp.float32),
        'w_in': nrm(ks[2], (DEPTH, D_MODEL, d_in), jnp.float32) * D_MODEL ** -0.5,
        'cmp_k_pe': 0.1 * nrm(ks[3], (DEPTH, CMP_BLOCK, HEAD_DIM), jnp.float32),
        'cmp_k_w1': nrm(ks[4], (DEPTH, f, CMP_HIDDEN), jnp.float32) * f ** -0.5,
        'cmp_k_w2': nrm(ks[5], (DEPTH, CMP_HIDDEN, HEAD_DIM), jnp.float32) * CMP_HIDDEN ** -0.5,
        'cmp_v_pe': 0.1 * nrm(ks[6], (DEPTH, CMP_BLOCK, HEAD_DIM), jnp.float32),
        'cmp_v_w1': nrm(ks[7], (DEPTH, f, CMP_HIDDEN), jnp.float32) * f ** -0.5,
        'cmp_v_w2': nrm(ks[8], (DEPTH, CMP_HIDDEN, HEAD_DIM), jnp.float32) * CMP_HIDDEN ** -0.5,
        'out_g_a': 1.0 + 0.02 * nrm(ks[9], (DEPTH, D_A), jnp.float32),
        'out_g_b': 1.0 + 0.02 * nrm(ks[10], (DEPTH, D_B), jnp.float32),
        'w_out': nrm(ks[11], (DEPTH, D_MIX, D_MODEL), jnp.float32) * D_MIX ** -0.5,
        'final_g': 1.0 + 0.02 * nrm(ks[12], (D_MODEL,), jnp.float32),
    }


def reference(x, norm_g, w_in, cmp_k_pe, cmp_k_w1, cmp_k_w2, cmp_v_pe, cmp_v_w1, cmp_v_w2,
              out_g_a, out_g_b, w_out, final_g):
    B, S, _ = x.shape
    slopes_a = _alibi_slopes(N_HEADS_A)
    slopes_b = _alibi_slopes(N_HEADS_B)

    def heads(t, n):
        return t.astype(jnp.float32).reshape(B, S, n, HEAD_DIM)

    for l in range(DEPTH):
        h = _rmsnorm(x, norm_g[l])
        proj = h @ w_in[l]
        (q_a, k_a, v_a, z_a, q_b, kc, vc, ksl, vsl, kwn, vwn, z_b, g_b) = jnp.split(
            proj, _split_points(), axis=-1)
        o_a = _dilated_mixture(heads(q_a, N_HEADS_A) * (HEAD_DIM ** -0.5), heads(k_a, N_HEADS_A),
                               heads(v_a, N_HEADS_A), slopes_a).reshape(B, S, D_A)
        gates = jax.nn.sigmoid(g_b.astype(jnp.float32)).reshape(B, S, N_HEADS_B, 3)
        o_b = _nsa(heads(q_b, N_HEADS_B), heads(kc, NSA_KV_HEADS), heads(vc, NSA_KV_HEADS),
                   heads(ksl, NSA_KV_HEADS), heads(vsl, NSA_KV_HEADS),
                   heads(kwn, NSA_KV_HEADS), heads(vwn, NSA_KV_HEADS), gates, slopes_b,
                   cmp_k_pe[l], cmp_k_w1[l], cmp_k_w2[l], cmp_v_pe[l], cmp_v_w1[l], cmp_v_w2[l])
        y_a = _rmsnorm(o_a, out_g_a[l]) * jax.nn.silu(z_a.astype(jnp.float32))
        y_b = _rmsnorm(o_b, out_g_b[l]) * jax.nn.silu(z_b.astype(jnp.float32))
        y = jnp.concatenate([y_a, y_b], axis=-1).astype(x.dtype) @ w_out[l]
        x = x + y.astype(x.dtype)
    return _rmsnorm(x, final_g)
```

```python
import contextlib
import numpy as np
import ml_dtypes
import concourse.bass as bass
import concourse.mybir as mybir
from concourse.bass_utils import run_bass_kernel_spmd

F32 = mybir.dt.float32
BF16 = mybir.dt.bfloat16
AF = mybir.ActivationFunctionType
ALU = mybir.AluOpType
AX = mybir.AxisListType

NCORES = 8
D = 4096
SEQ = 16384
TOK = SEQ // NCORES
HD = 128
NHA = 16
NHB = 16
DIN = 13872
DEPTH = 4
EPS = 1e-6
QSCALE = HD ** -0.5
C_QA, C_KA, C_VA, C_ZA, C_QB = 0, 2048, 4096, 6144, 8192
C_KC, C_VC, C_KS, C_VS, C_KW, C_VW, C_ZB, C_G = 10240, 10496, 10752, 11008, 11264, 11520, 11776, 13824


class Sched:
    def __init__(self, nc):
        self.nc = nc
        self.eng = {"pe": nc.tensor, "dve": nc.vector, "act": nc.scalar, "pool": nc.gpsimd, "sp": nc.sync}
        self.esem, self.ecnt = {}, {}
        self.seen = {k: {} for k in self.eng}
        self.bufs = {}
        self.dsem, self.dcnt = {}, {}
        self._ctx = contextlib.ExitStack()
        for k in self.eng:
            self.esem[k] = self._ctx.enter_context(nc.semaphore("s_" + k))
            self.ecnt[k] = 0

    def _buf(self, b):
        st = self.bufs.get(b)
        if st is None:
            st = self.bufs[b] = [{}, {}]
        return st

    def _deps(self, reads, writes):
        d = {}

        def add(m):
            for k, sv in m.items():
                if k not in d or d[k][1] < sv[1]:
                    d[k] = sv

        for b in reads:
            add(self._buf(b)[0])
        for b in writes:
            w, r = self._buf(b)
            add(w)
            add(r)
        return d

    def _wait(self, e, deps, skip_self=False):
        seen = self.seen[e]
        for k, (s, v) in deps.items():
            if skip_self and k == "E" + e:
                continue
            if seen.get(k, -1) >= v:
                continue
            self.eng[e].wait_ge(s, v)
            seen[k] = v

    def _commit(self, reads, writes, key, sem, val):
        for b in reads:
            r = self._buf(b)[1]
            if key not in r or r[key][1] < val:
                r[key] = (sem, val)
        for b in writes:
            st = self._buf(b)
            st[0] = {key: (sem, val)}
            st[1] = {}

    def op(self, e, fn, reads=(), writes=()):
        deps = self._deps(reads, writes)
        self._wait(e, deps, skip_self=(e == "pe"))
        ins = fn()
        self.ecnt[e] += 1
        ins.then_inc(self.esem[e], 1)
        self._commit(reads, writes, "E" + e, self.esem[e], self.ecnt[e])
        return ins

    def dma(self, q, out, in_, reads, writes, tag, **kw):
        if tag not in self.dsem:
            self.dsem[tag] = self._ctx.enter_context(self.nc.semaphore("d_" + str(tag)))
            self.dcnt[tag] = 0
        deps = self._deps(reads, writes)
        self._wait(q, deps)
        ins = self.eng[q].dma_start(out=out, in_=in_, **kw)
        self.dcnt[tag] += 16
        ins.then_inc(self.dsem[tag], 16)
        self._commit(reads, writes, "D" + str(tag), self.dsem[tag], self.dcnt[tag])
        return ins

    def finish(self, e, bufs):
        self._wait(e, self._deps(bufs, bufs))

    def drain(self, engines=None):
        deps = {}
        for k in self.eng:
            if self.ecnt[k] > 0:
                deps["E" + k] = (self.esem[k], self.ecnt[k])
        for t, s_ in self.dsem.items():
            if self.dcnt[t] > 0:
                deps["D" + str(t)] = (s_, self.dcnt[t])
        for e in (engines or self.eng):
            self._wait(e, deps)


def emit_p1(nc, S, x, ng, win, ident, o, pfx="p1"):
    TP = 1024
    NT = TP // 128
    KC = D // 128
    with contextlib.ExitStack() as es:
        sb = lambda name, shape, dt: es.enter_context(nc.sbuf_tensor(pfx + name, shape, dt))
        hT = sb("hT", [128, KC, TP], BF16)
        wb = [sb(f"wb{i}", [128, KC, 512], BF16) for i in range(2)]
        xt = [sb(f"xt{i}", [128, D], F32) for i in range(2)]
        xs = sb("xs", [128, D], BF16)
        gT = sb("gT", [128, KC], F32)
        idb = sb("idb", [128, 128], BF16)
        ss = sb("ss", [128, 2], F32)
        stF = [sb(f"stF{i}", [128, 512], BF16) for i in range(3)]
        stV = [sb(f"stV{i}", [128, 4, 129], BF16) for i in range(3)]
        stZ = [sb(f"stZ{i}", [128, 512], F32) for i in range(3)]
        acc = [es.enter_context(nc.psum_tensor(pfx + f"acc{i}", [128, 512], F32)) for i in range(6)]
        tps = [es.enter_context(nc.psum_tensor(pfx + f"tp{i}", [128, 512], BF16)) for i in range(2)]
        B = lambda n: pfx + n

        nc_ = nc
        S.dma("sp", idb[:], ident[:, :], reads=[], writes=[B("idb")], tag=B("idb"))
        with nc.allow_non_contiguous_dma(reason="tiny gain vector"):
            S.dma("sp", gT[:], ng.rearrange("(c p) -> p c", p=128), reads=[], writes=[B("gT")], tag=B("gT"))
        for i in range(3):
            S.op("dve", lambda: nc.vector.memset(stV[i][:], 1.0), writes=[B(f"stV{i}")])

        cnt = {"acc": 0, "tp": 0, "F": 0, "V": 0, "Z": 0, "w": 0, "ev": 0}

        def evac(dst, src, reads, writes, scale=None):
            cnt["ev"] += 1
            if cnt["ev"] % 2 == 0:
                if scale is None:
                    S.op("dve", lambda: nc.vector.tensor_copy(dst, src), reads=reads, writes=writes)
                else:
                    S.op("dve", lambda: nc.vector.tensor_scalar_mul(dst, src, scale), reads=reads, writes=writes)
            else:
                S.op("act", lambda: nc.scalar.activation(dst, src, AF.Copy, scale=(1.0 if scale is None else scale)),
                     reads=reads, writes=writes)

        for p in range(TOK // TP):
            t0 = p * TP
            for tt in range(NT):
                xb = xt[tt % 2]
                xbn = B(f"xt{tt%2}")
                S.dma("sp", xb[:], x[t0 + tt * 128:t0 + (tt + 1) * 128, :], reads=[], writes=[xbn], tag=xbn)
                S.op("act", lambda: nc.scalar.activation(xs[:], xb[:], AF.Square, accum_out=ss[:, 0:1]),
                     reads=[xbn], writes=[B("xs"), B("ss")])
                S.op("dve", lambda: nc.vector.tensor_scalar(ss[:, 1:2], ss[:, 0:1], 1.0 / D, EPS, ALU.mult, ALU.add),
                     reads=[B("ss")], writes=[B("ss")])
                S.op("act", lambda: nc.scalar.activation(ss[:, 1:2], ss[:, 1:2], AF.Sqrt),
                     reads=[B("ss")], writes=[B("ss")])
                S.op("dve", lambda: nc.vector.reciprocal(ss[:, 1:2], ss[:, 1:2]),
                     reads=[B("ss")], writes=[B("ss")])
                S.op("act", lambda: nc.scalar.activation(xs[:], xb[:], AF.Copy, scale=ss[:, 1:2]),
                     reads=[xbn, B("ss")], writes=[B("xs")])
                for k4 in range(KC // 4):
                    tpi = cnt["tp"] % 2
                    cnt["tp"] += 1
                    tp = tps[tpi]
                    for j in range(4):
                        kc = k4 * 4 + j
                        S.op("pe", lambda: nc.tensor.transpose(tp[:, j * 128:(j + 1) * 128], xs[:, kc * 128:(kc + 1) * 128], idb[:]),
                             reads=[B("xs"), B("idb")], writes=[B(f"tp{tpi}")])
                    for j in range(4):
                        kc = k4 * 4 + j
                        dst = hT[:, kc, tt * 128:(tt + 1) * 128]
                        src = tp[:, j * 128:(j + 1) * 128]
                        if j % 2 == 0:
                            S.op("dve", lambda: nc.vector.tensor_scalar_mul(dst, src, gT[:, kc:kc + 1]),
                                 reads=[B(f"tp{tpi}"), B("gT")], writes=[B("hT")])
                        else:
                            S.op("act", lambda: nc.scalar.activation(dst, src, AF.Copy, scale=gT[:, kc:kc + 1]),
                                 reads=[B(f"tp{tpi}"), B("gT")], writes=[B("hT")])

            def load_w(c0, ncols):
                wi = cnt["w"] % 2
                cnt["w"] += 1
                S.dma("pool", wb[wi][:, :, 0:ncols], win[:, c0:c0 + ncols].rearrange("(kc p) n -> p kc n", p=128),
                      reads=[], writes=[B(f"wb{wi}")], tag=B(f"wb{wi}"))
                return wb[wi], B(f"wb{wi}")

            def next_acc():
                ai = cnt["acc"] % 6
                cnt["acc"] += 1
                return acc[ai], B(f"acc{ai}")

            def feat_tile(w, wn, wc0, dst_fn, scale):
                for half in range(TP // 512):
                    a, an = next_acc()
                    for kc in range(KC):
                        S.op("pe", lambda: nc.tensor.matmul(a[:], w[:, kc, wc0:wc0 + 128], hT[:, kc, half * 512:(half + 1) * 512],
                                                            start=(kc == 0), stop=(kc == KC - 1)),
                             reads=[wn, B("hT")], writes=[an])
                    si = cnt["F"] % 3
                    cnt["F"] += 1
                    evac(stF[si][:], a[:], [an], [B(f"stF{si}")], scale)
                    S.dma("sp", dst_fn(t0 + half * 512, 512), stF[si][:], reads=[B(f"stF{si}")], writes=[], tag=B(f"stF{si}"))

            def tok_tiles(w, wn, wc0, ncols, kind, dst_fn):
                for tt in range(NT):
                    a, an = next_acc()
                    for kc in range(KC):
                        S.op("pe", lambda: nc.tensor.matmul(a[:, 0:ncols], hT[:, kc, tt * 128:(tt + 1) * 128], w[:, kc, wc0:wc0 + ncols],
                                                            start=(kc == 0), stop=(kc == KC - 1)),
                             reads=[wn, B("hT")], writes=[an])
                    tok0 = t0 + tt * 128
                    if kind == "V":
                        si = cnt["V"] % 3
                        cnt["V"] += 1
                        nh = ncols // 128
                        evac(stV[si][:, 0:nh, 0:128], a[:, 0:ncols].rearrange("p (h d) -> p h d", d=128),
                             [an], [B(f"stV{si}")])
                        S.dma("sp", dst_fn(tok0), stV[si][:, 0:nh, :], reads=[B(f"stV{si}")], writes=[], tag=B(f"stV{si}"))
                    else:
                        si = cnt["Z"] % 3
                        cnt["Z"] += 1
                        evac(stZ[si][:, 0:ncols], a[:, 0:ncols], [an], [B(f"stZ{si}")])
                        S.dma("sp", dst_fn(tok0), stZ[si][:, 0:ncols], reads=[B(f"stZ{si}")], writes=[], tag=B(f"stZ{si}"))

            for (c0, dst, scale) in ((C_QA, o["qaT"], QSCALE), (C_KA, o["kaT"], None), (C_QB, o["qbT"], QSCALE)):
                for ch in range(4):
                    w, wn = load_w(c0 + ch * 512, 512)
                    for sub in range(4):
                        h = ch * 4 + sub
                        feat_tile(w, wn, sub * 128, lambda tk, n, h=h, dst=dst: dst[h, :, tk:tk + n], scale)
            w, wn = load_w(C_KC, 512)
            for sub in range(4):
                feat_tile(w, wn, sub * 128, lambda tk, n, sub=sub: o["kvT"][sub // 2, sub % 2, :, tk:tk + n], None)
            w, wn = load_w(C_KS, 512)
            for sub in range(2):
                feat_tile(w, wn, sub * 128, lambda tk, n, sub=sub: o["kvT"][2, sub, :, tk:tk + n], None)
            tok_tiles(w, wn, 256, 256, "V", lambda tok0: o["vsw"][0, tok0:tok0 + 128, :, :])
            w, wn = load_w(C_KW, 512)
            for sub in range(2):
                feat_tile(w, wn, sub * 128, lambda tk, n, sub=sub: o["kvT"][3, sub, :, tk:tk + n], None)
            tok_tiles(w, wn, 256, 256, "V", lambda tok0: o["vsw"][1, tok0:tok0 + 128, :, :])
            for ch in range(4):
                w, wn = load_w(C_VA + ch * 512, 512)
                tok_tiles(w, wn, 0, 512, "V", lambda tok0, ch=ch: o["va"][tok0:tok0 + 128, ch * 4:(ch + 1) * 4, :])
            for (c0, zoff) in ((C_ZA, 0), (C_ZB, 2048)):
                for ch in range(4):
                    w, wn = load_w(c0 + ch * 512, 512)
                    tok_tiles(w, wn, 0, 512, "Z",
                              lambda tok0, ch=ch, zoff=zoff: o["z"][tok0:tok0 + 128, zoff + ch * 512:zoff + (ch + 1) * 512])
            w, wn = load_w(C_G, 48)
            tok_tiles(w, wn, 0, 48, "Z", lambda tok0: o["gl"][tok0:tok0 + 128, :])

        outbufs = [B(f"stF{i}") for i in range(3)] + [B(f"stV{i}") for i in range(3)] + [B(f"stZ{i}") for i in range(3)]
        return outbufs


def p1_out_specs():
    return {
        "qaT": ([NHA, 128, TOK], BF16), "kaT": ([NHA, 128, TOK], BF16), "va": ([TOK, NHA, 129], BF16),
        "qbT": ([NHB, 128, TOK], BF16), "kvT": ([4, 2, 128, TOK], BF16), "vsw": ([2, TOK, 2, 129], BF16),
        "z": ([TOK, D], F32), "gl": ([TOK, 48], F32),
    }


def build_p1_only():
    nc = bass.Bass("TRN2", target_bir_lowering=False)
    x = nc.dram_tensor("x", [TOK, D], F32, kind="ExternalInput").ap()
    ng = nc.dram_tensor("ng", [D], F32, kind="ExternalInput").ap()
    win = nc.dram_tensor("win", [D, DIN], F32, kind="ExternalInput").ap()
    ident = nc.dram_tensor("ident", [128, 128], BF16, kind="ExternalInput").ap()
    o = {k: nc.dram_tensor(k, shp, dt, kind="ExternalOutput").ap() for k, (shp, dt) in p1_out_specs().items()}
    S = Sched(nc)
    bufs = emit_p1(nc, S, x, ng, win, ident, o)
    S.finish("sp", bufs)
    return nc


SLOPES = [2.0 ** (-(k + 1) / 2.0) for k in range(16)]
NEGBIG = -30000.0
POSBIG = 1.0e9
NEED = [40.0 / s for s in SLOPES]
NBS = [min(int(np.ceil(n / 128.0)) + 2, 97) for n in NEED]
NBC = [min(int((n + 16 + 2063) // 2048) + 1, 7) for n in NEED]
NPADR = 6
WTOK = (NPADR + 1) * TOK
NCMP = WTOK // 16
NJW = WTOK // 64


def a_type(off):
    return 0 if off == 0 else 1 if off == 1 else 2 if off in (2, 3) else 3 if off == 4 else 4 if off < 16 else 5


def make_tables(core):
    k = np.arange(128)[:, None].astype(np.float64)
    q = np.arange(128)[None, :].astype(np.float64)
    dqk = q - k
    t = {}
    dq = np.zeros((3, 128, 128), np.float32)
    dq[0] = dqk
    dq[1] = np.where(dqk >= 0, dqk, POSBIG)
    dq[2] = np.where(dqk < 0, dqk, POSBIG)
    t["dqk"] = dq
    lm = np.zeros((6, 128, 128), np.float32)
    for ty, off in enumerate((0, 1, 2, 4, 5, 16)):
        dl = 128 * off + dqk
        mult = ((dl >= 0) & (dl <= 128)).astype(np.float64)
        mult += ((dl >= 0) & (dl % 4 == 0) & (dl <= 512))
        mult += ((dl >= 0) & (dl % 16 == 0) & (dl <= 2048))
        if ty == 2:
            dl3 = 128 * 3 + dqk
            m3 = ((dl3 % 4 == 0) & (dl3 <= 512)).astype(np.float64) + ((dl3 % 16 == 0) & (dl3 <= 2048))
            assert np.array_equal(m3, mult)
        lm[ty] = np.where(mult > 0, np.log(np.maximum(mult, 1)), NEGBIG)
    t["lm"] = lm
    cd = np.zeros((2, 16, 128, 128), np.float32)
    for i in range(16):
        d0 = q - 16 * k + 128 * i - 31
        cd[0, i] = np.where(d0 >= 0, d0, POSBIG)
        d1 = d0 + 2048
        cd[1, i] = np.where(d1 >= 0, d1, POSBIG)
    t["cd"] = cd
    ov = np.zeros((128, 33), np.float32)
    for m in range(33):
        ov[:, m] = ((k[:, 0] >= 4 * m - 1) & (k[:, 0] <= 4 * m + 3))
    t["ov"] = ov.astype(ml_dtypes.bfloat16)
    wt = np.zeros((128, 8192), np.float32)
    u = np.arange(8192)
    wt[u // 64, u] = -NEGBIG
    t["wt"] = wt.astype(ml_dtypes.bfloat16)
    fb = np.zeros((16, 128, 256), np.float32)
    jw = np.arange(256)[None, :]
    for i in range(16):
        curw = 192 + 2 * i + (np.arange(128)[:, None] >= 64)
        jabs = jw - 192 + 32 * core
        valid = (jabs >= 0) & (jw <= curw) & (jw < NJW)
        forced = valid & ((jabs == 0) | ((curw - jw) < 2))
        fb[i] = np.where(forced, 1000.0 + jw, np.where(valid, 0.0, -1000.0 - jw))
    t["fb"] = fb
    a = np.arange(NCMP)
    cval = ((a - NPADR * 128 + 128 * core) >= 0).astype(np.float32)
    t["cval"] = np.ascontiguousarray(cval.reshape(7, 128).T)
    t["ident"] = np.eye(128, dtype=np.float32).astype(ml_dtypes.bfloat16)
    return t


TABLE_SPECS = {"dqk": ([3, 128, 128], F32), "lm": ([6, 128, 128], F32), "cd": ([2, 16, 128, 128], F32),
               "ov": ([128, 33], BF16), "wt": ([128, 8192], BF16), "fb": ([16, 128, 256], F32),
               "cval": ([128, 7], F32), "ident": ([128, 128], BF16)}


def emit_p2(nc, S, I, O, pfx="p2", do_a=True, do_b=True, b_groups=(0, 1), b_qblocks=tuple(range(16)), b_stage=9, b_sub=9):
    B = lambda n: pfx + n
    with contextlib.ExitStack() as es:
        sb = lambda name, shape, dt: es.enter_context(nc.sbuf_tensor(pfx + name, shape, dt))
        ps = lambda name, shape, dt: es.enter_context(nc.psum_tensor(pfx + name, shape, dt))
        dqk = sb("dqk", [128, 3, 128], F32)
        idb = sb("idb", [128, 128], BF16)
        tmpt = [sb(f"tmp{i}", [128, 128], F32) for i in range(4)]
        ptt = [sb(f"pt{i}", [128, 128], BF16) for i in range(4)]
        rd = [sb(f"rd{i}", [128, 4], F32) for i in range(2)]
        obank = [ps(f"ob{i}", [128, 512], F32) for i in range(2)]
        stb = [ps(f"st{i}", [128, 4, 128], F32) for i in range(2)]
        xbank = [ps(f"xb{i}", [128, 512], F32) for i in range(2)]
        selTp = ps("selTp", [128, 2, 128], BF16)
        with nc.allow_non_contiguous_dma(reason="small tables"):
            S.dma("sp", dqk[:], I["dqk"].rearrange("t k q -> k t q"), reads=[], writes=[B("dqk")], tag=B("dqk"))
        S.dma("sp", idb[:], I["ident"][:, :], reads=[], writes=[B("idb")], tag=B("idb"))
        cnt = {"st": 0, "tp": 0, "ob": 0, "rd": 0, "otmp": 0}

        def unit(kT_ap, q_ap, v_ap, oacc, oname, first, last, rd_list, tab_fn, cbias, ncols=129, mask=None, extra=None):
            si = cnt["st"] % 8
            cnt["st"] += 1
            st = stb[si // 4][:, si % 4, :]
            stn = B(f"st{si}")
            S.op("pe", lambda: nc.tensor.matmul(st, kT_ap, q_ap, start=True, stop=(mask is None)),
                 reads=rd_list, writes=[stn])
            if mask is not None:
                S.op("pe", lambda: nc.tensor.matmul(st, mask[0], mask[1], start=False, stop=True),
                     reads=mask[2], writes=[stn])
            ti = cnt["tp"] % 4
            cnt["tp"] += 1
            tab_fn(tmpt[ti][:], st, stn, B(f"tmp{ti}"))
            S.op("act", lambda: nc.scalar.activation(ptt[ti][:], tmpt[ti][:], AF.Exp, bias=float(cbias), scale=1.0),
                 reads=[B(f"tmp{ti}")], writes=[B(f"pt{ti}")])
            S.op("pe", lambda: nc.tensor.matmul(oacc[:, 0:ncols], ptt[ti][:], v_ap, start=first, stop=last, skip_group_check=(extra is not None)),
                 reads=[B(f"pt{ti}")] + rd_list, writes=[oname])
            if extra is not None:
                S.op("pe", lambda: nc.tensor.matmul(extra[0], ptt[ti][:], extra[1], start=False, stop=last, skip_group_check=True),
                     reads=[B(f"pt{ti}")] + extra[2], writes=[oname])

        def next_ob():
            oi = cnt["ob"] % 2
            cnt["ob"] += 1
            return obank[oi], B(f"ob{oi}")

        if do_a:
            with contextlib.ExitStack() as ea:
                sa = lambda name, shape, dt: ea.enter_context(nc.sbuf_tensor(pfx + name, shape, dt))
                lm = sa("lm", [128, 6, 128], F32)
                kTa = [sa(f"kTa{i}", [128, 2, TOK], BF16) for i in range(2)]
                vA = [sa(f"vA{i}", [128, 32, 129], BF16) for i in range(2)]
                qA = [sa(f"qA{i}", [128, TOK], BF16) for i in range(2)]
                ta = [sa(f"ta{i}", [128, 6, 128], F32) for i in range(2)]
                ost = [sa(f"osta{i}", [128, 128], F32) for i in range(3)]
                with nc.allow_non_contiguous_dma(reason="small tables"):
                    S.dma("sp", lm[:], I["lm"].rearrange("t k q -> k t q"), reads=[], writes=[B("lm")], tag=B("lm"))
                no = 0
                for h in range(NHA):
                    hb = h % 2
                    sl = SLOPES[h]
                    S.dma("sp", kTa[hb][:], I["kaTw"][:, h, :, :].rearrange("r d t -> d r t"),
                          reads=[], writes=[B(f"kTa{hb}")], tag=B(f"kTa{hb}"))
                    for r in range(2):
                        S.dma("sp", vA[hb][:, r * 16:(r + 1) * 16, :], I["vaw"][r, :, h, :].rearrange("(b p) e -> p b e", p=128),
                              reads=[], writes=[B(f"vA{hb}")], tag=B(f"vA{hb}"))
                    S.dma("sp", qA[hb][:], I["qaT"][h, :, :], reads=[], writes=[B(f"qA{hb}")], tag=B(f"qA{hb}"))
                    for ty in range(6):
                        S.op("dve", lambda: nc.vector.scalar_tensor_tensor(ta[hb][:, ty, :], dqk[:, 0, :], -sl, lm[:, ty, :], ALU.mult, ALU.add),
                             reads=[B("dqk"), B("lm")], writes=[B(f"ta{hb}")])
                    kflat = kTa[hb][:].rearrange("d r t -> d (r t)")
                    for i in range(16):
                        oacc, on = next_ob()
                        for off in range(17):
                            wbk = 16 + i - off
                            ty = a_type(off)

                            def tab(tmp, st, stn, tmpn, ty=ty):
                                S.op("dve", lambda: nc.vector.tensor_tensor(tmp, st, ta[hb][:, ty, :], ALU.add),
                                     reads=[stn, B(f"ta{hb}")], writes=[tmpn])

                            unit(kflat[:, wbk * 128:(wbk + 1) * 128], qA[hb][:, i * 128:(i + 1) * 128], vA[hb][:, wbk, :],
                                 oacc, on, off == 0, off == 16, [B(f"kTa{hb}"), B(f"qA{hb}"), B(f"vA{hb}")], tab, -sl * 128 * off)
                        ri = cnt["rd"] % 2
                        cnt["rd"] += 1
                        S.op("dve", lambda: nc.vector.reciprocal(rd[ri][:, 0:1], oacc[:, 128:129]), reads=[on], writes=[B(f"rd{ri}")])
                        oi = no % 3
                        no += 1
                        S.op("act", lambda: nc.scalar.activation(ost[oi][:], oacc[:, 0:128], AF.Copy, scale=rd[ri][:, 0:1]),
                             reads=[on, B(f"rd{ri}")], writes=[B(f"osta{oi}")])
                        S.dma("sp", O["oa"][i * 128:(i + 1) * 128, h * 128:(h + 1) * 128], ost[oi][:],
                              reads=[B(f"osta{oi}")], writes=[], tag=B(f"osta{oi}"))

        S.drain()
        if do_b:
            with contextlib.ExitStack() as eb:
                sB = lambda name, shape, dt: eb.enter_context(nc.sbuf_tensor(pfx + name, shape, dt))
                cd = sB("cd", [128, 32, 128], F32)
                fb = sB("fb", [128, 16, 256], F32)
                ov = sB("ov", [128, 33], BF16)
                wt = sB("wt", [128, 8192], BF16)
                cval = sB("cval", [128, 7], F32)
                ksT = sB("ksT", [128, 7, TOK], BF16)
                vs = sB("vs", [128, 112, 129], BF16)
                kwT = sB("kwT", [128, 20 * 128], BF16)
                vw = sB("vw", [128, 20, 129], BF16)
                kcmpT = sB("kcmpT", [128, NCMP], BF16)
                vcmp = sB("vcmp", [128, 7, 129], BF16)
                qB = [sB(f"qB{i}", [128, 8, 128], BF16) for i in range(2)]
                glt = [sB(f"gl{i}", [128, 48], F32) for i in range(2)]
                gs = sB("gs", [128, 48], F32)
                impa = sB("impa", [128, 256], F32)
                work = sB("work", [128, 256], F32)
                m8 = sB("m8", [128, 16], F32)
                sel = sB("sel", [128, 256], F32)
                selb = sB("selb", [128, 256], BF16)
                selT = sB("selT", [128, 2, 128], BF16)
                obst = [sB(f"obst{i}", [128, 8, 128], F32) for i in range(2)]
                otmp = [sB(f"otmp{i}", [128, 128], F32) for i in range(2)]
                with nc.allow_non_contiguous_dma(reason="small tables"):
                    S.dma("sp", cd[:], I["cd"].rearrange("a i k q -> k (a i) q"), reads=[], writes=[B("cd")], tag=B("cd"))
                    S.dma("sp", fb[:], I["fb"].rearrange("i q j -> q i j"), reads=[], writes=[B("fb")], tag=B("fb"))
                S.dma("sp", ov[:], I["ov"][:, :], reads=[], writes=[B("ov")], tag=B("ov"))
                S.dma("sp", wt[:], I["wt"][:, :], reads=[], writes=[B("wt")], tag=B("wt"))
                S.dma("sp", cval[:], I["cval"][:, :], reads=[], writes=[B("cval")], tag=B("cval"))
                ncoef = 0
                nobst = 0
                for g in b_groups:
                    S.dma("sp", ksT[:], I["kvTw"][:, 2, g, :, :].rearrange("r d t -> d r t"), reads=[], writes=[B("ksT")], tag=B("ksT"))
                    for r in range(7):
                        S.dma("sp", vs[:, r * 16:(r + 1) * 16, :], I["vsww"][r, 0, :, g, :].rearrange("(b p) e -> p b e", p=128),
                              reads=[], writes=[B("vs")], tag=B("vs"))
                    S.dma("sp", kwT[:, 0:512], I["kvTw"][5, 3, g, :, TOK - 512:TOK], reads=[], writes=[B("kwT")], tag=B("kwT"))
                    S.dma("sp", kwT[:, 512:], I["kvTw"][6, 3, g, :, :], reads=[], writes=[B("kwT")], tag=B("kwT"))
                    S.dma("sp", vw[:, 0:4, :], I["vsww"][5, 1, TOK - 512:TOK, g, :].rearrange("(b p) e -> p b e", p=128),
                          reads=[], writes=[B("vw")], tag=B("vw"))
                    S.dma("sp", vw[:, 4:20, :], I["vsww"][6, 1, :, g, :].rearrange("(b p) e -> p b e", p=128),
                          reads=[], writes=[B("vw")], tag=B("vw"))
                    if b_stage < 1:
                        continue
                    with contextlib.ExitStack() as ec:
                        sc = lambda name, shape, dt: ec.enter_context(nc.sbuf_tensor(pfx + name + str(g), shape, dt))
                        tT = sc("tT", [128, 7, TOK], BF16)
                        w1 = sc("w1", [128, 32, 256], BF16)
                        w2 = sc("w2", [128, 2, 128], BF16)
                        peT = sc("peT", [128, 32], BF16)
                        pesb = sc("pesb", [32, 128], BF16)
                        b1 = sc("b1", [128, 2], F32)
                        u = sc("u", [128, 448], F32)
                        u2 = sc("u2", [128, 448], F32)
                        hb_ = sc("hb", [128, 2, NCMP], BF16)
                        S.op("pool", lambda: nc.gpsimd.memset(hb_[:], 0.0), writes=[B("hb")])
                        for which in range(2):
                            S.dma("sp", tT[:], I["kvTw"][:, which, g, :, :].rearrange("r d t -> d r t"), reads=[], writes=[B("tT")], tag=B("tT"))
                            S.dma("pool", w1[:], I["cw1"][which].rearrange("(l p) n -> p l n", p=128), reads=[], writes=[B("w1")], tag=B("w1"))
                            S.dma("pool", w2[:], I["cw2"][which].rearrange("(c p) n -> p c n", p=128), reads=[], writes=[B("w2")], tag=B("w2"))
                            S.dma("pool", pesb[:], I["cpe"][which], reads=[], writes=[B("pesb")], tag=B("pesb"))
                            S.op("pe", lambda: nc.tensor.transpose(selTp[:, 0, 0:32], pesb[:, :], idb[0:32, 0:32]),
                                 reads=[B("pesb"), B("idb")], writes=[B("selTp")])
                            S.op("dve", lambda: nc.vector.tensor_copy(peT[:], selTp[:, 0, 0:32]), reads=[B("selTp")], writes=[B("peT")])
                            tflat = tT[:].rearrange("d r t -> d (r t)")
                            for hc in range(2):
                                xb_ = xbank[0]
                                for l in range(32):
                                    S.op("pe", lambda: nc.tensor.matmul(xb_[:, 0:1], w1[:, l, hc * 128:(hc + 1) * 128], peT[:, l:l + 1],
                                                                        start=(l == 0), stop=(l == 31)),
                                         reads=[B("w1"), B("peT")], writes=[B("xb0")])
                                S.op("dve", lambda: nc.vector.tensor_copy(b1[:, hc:hc + 1], xb_[:, 0:1]), reads=[B("xb0")], writes=[B("b1")])
                            for cb in range(2):
                                c0 = cb * 448
                                ncol = 448 if cb == 0 else 447
                                for hc in range(2):
                                    xb_ = xbank[1]
                                    for l in range(32):
                                        rhs = tflat.rearrange("d (c s) -> d c s", s=16)[:, c0 + l // 16:c0 + l // 16 + ncol, l % 16]
                                        S.op("pe", lambda: nc.tensor.matmul(xb_[:, 0:ncol], w1[:, l, hc * 128:(hc + 1) * 128], rhs,
                                                                            start=(l == 0), stop=(l == 31)),
                                             reads=[B("w1"), B("tT")], writes=[B("xb1")])
                                    S.op("dve", lambda: nc.vector.tensor_scalar(u[:, 0:ncol], xb_[:, 0:ncol], b1[:, hc:hc + 1], None, ALU.add),
                                         reads=[B("xb1"), B("b1")], writes=[B("u")])
                                    S.op("dve", lambda: nc.vector.tensor_tensor(u2[:, 0:ncol], u[:, 0:ncol], u[:, 0:ncol], ALU.mult),
                                         reads=[B("u")], writes=[B("u2")])
                                    S.op("dve", lambda: nc.vector.tensor_scalar(u2[:, 0:ncol], u2[:, 0:ncol], 0.044715, 1.0, ALU.mult, ALU.add),
                                         reads=[B("u2")], writes=[B("u2")])
                                    S.op("dve", lambda: nc.vector.tensor_tensor(u2[:, 0:ncol], u2[:, 0:ncol], u[:, 0:ncol], ALU.mult),
                                         reads=[B("u2"), B("u")], writes=[B("u2")])
                                    S.op("act", lambda: nc.scalar.activation(u2[:, 0:ncol], u2[:, 0:ncol], AF.Sigmoid, scale=1.5957691216057308),
                                         reads=[B("u2")], writes=[B("u2")])
                                    S.op("dve", lambda: nc.vector.tensor_tensor(hb_[:, hc, c0:c0 + ncol], u2[:, 0:ncol], u[:, 0:ncol], ALU.mult),
                                         reads=[B("u2"), B("u")], writes=[B("hb")])
                            if which == 0:
                                for cb in range(2):
                                    xb_ = xbank[0]
                                    for hc in range(2):
                                        S.op("pe", lambda: nc.tensor.matmul(xb_[:, 0:448], w2[:, hc, :], hb_[:, hc, cb * 448:(cb + 1) * 448],
                                                                            start=(hc == 0), stop=(hc == 1)),
                                             reads=[B("w2"), B("hb")], writes=[B("xb0")])
                                    S.op("dve", lambda: nc.vector.tensor_copy(kcmpT[:, cb * 448:(cb + 1) * 448], xb_[:, 0:448]),
                                         reads=[B("xb0")], writes=[B("kcmpT")])
                            else:
                                for bb in range(7):
                                    xb_ = xbank[bb % 2]
                                    xn = B(f"xb{bb % 2}")
                                    for hc in range(2):
                                        S.op("pe", lambda: nc.tensor.matmul(xb_[:, 0:128], hb_[:, hc, bb * 128:(bb + 1) * 128], w2[:, hc, :],
                                                                            start=(hc == 0), stop=(hc == 1)),
                                             reads=[B("w2"), B("hb")], writes=[xn])
                                    S.op("dve", lambda: nc.vector.tensor_scalar_mul(vcmp[:, bb, 0:128], xb_[:, 0:128], cval[:, bb:bb + 1]),
                                         reads=[xn, B("cval")], writes=[B("vcmp")])
                                    S.op("dve", lambda: nc.vector.tensor_copy(vcmp[:, bb, 128:129], cval[:, bb:bb + 1]),
                                         reads=[B("cval")], writes=[B("vcmp")])
                    S.drain()
                    if b_stage < 2:
                        continue
                    ksflat = ksT[:].rearrange("d r t -> d (r t)")
                    for i in b_qblocks:
                        qi = (g * 16 + i) % 2
                        qn = B(f"qB{qi}")
                        S.dma("sp", qB[qi][:], I["qbT"][g * 8:(g + 1) * 8, :, i * 128:(i + 1) * 128].rearrange("h d t -> d h t"),
                              reads=[], writes=[qn], tag=qn)
                        S.dma("sp", glt[qi][:], I["gl"][i * 128:(i + 1) * 128, :], reads=[], writes=[B(f"gl{qi}")], tag=B(f"gl{qi}"))
                        S.op("act", lambda: nc.scalar.activation(gs[:], glt[qi][:], AF.Sigmoid), reads=[B(f"gl{qi}")], writes=[B("gs")])
                        S.op("dve", lambda: nc.vector.tensor_copy(impa[:], fb[:, i, :]), reads=[B("fb")], writes=[B("impa")])
                        obi = nobst % 2
                        nobst += 1
                        ob_t = obst[obi]
                        obn = B(f"obst{obi}")

                        def finish_head(oacc, on, hl, br, first_branch, imp_lo=None):
                            h = g * 8 + hl
                            ri = cnt["rd"] % 2
                            cnt["rd"] += 1
                            rdn = B(f"rd{ri}")
                            S.op("dve", lambda: nc.vector.tensor_scalar(rd[ri][:, 2:3], oacc[:, 128:129], 1e-30, None, ALU.max),
                                 reads=[on], writes=[rdn])
                            S.op("dve", lambda: nc.vector.reciprocal(rd[ri][:, 0:1], rd[ri][:, 2:3]), reads=[rdn], writes=[rdn])
                            S.op("dve", lambda: nc.vector.scalar_tensor_tensor(rd[ri][:, 0:1], rd[ri][:, 2:3], 1e-20, rd[ri][:, 0:1], ALU.is_gt, ALU.mult),
                                 reads=[rdn], writes=[rdn])
                            S.op("dve", lambda: nc.vector.tensor_tensor(rd[ri][:, 1:2], rd[ri][:, 0:1], gs[:, 3 * h + br:3 * h + br + 1], ALU.mult),
                                 reads=[rdn, B("gs")], writes=[rdn])
                            if imp_lo is not None:
                                S.op("act", lambda: nc.scalar.activation(work[:, imp_lo:225], oacc[:, 132 + imp_lo:132 + 225], AF.Copy, scale=rd[ri][:, 0:1]),
                                     reads=[on, rdn], writes=[B("work")])
                                S.op("pool", lambda: nc.gpsimd.tensor_tensor(impa[:, imp_lo:225], impa[:, imp_lo:225], work[:, imp_lo:225], ALU.add),
                                     reads=[B("work"), B("impa")], writes=[B("impa")])
                            if first_branch:
                                S.op("act", lambda: nc.scalar.activation(ob_t[:, hl, :], oacc[:, 0:128], AF.Copy, scale=rd[ri][:, 1:2]),
                                     reads=[on, rdn], writes=[obn])
                            else:
                                oi_ = cnt["otmp"] % 2
                                cnt["otmp"] += 1
                                S.op("act", lambda: nc.scalar.activation(otmp[oi_][:], oacc[:, 0:128], AF.Copy, scale=rd[ri][:, 1:2]),
                                     reads=[on, rdn], writes=[B(f"otmp{oi_}")])
                                S.op("pool", lambda: nc.gpsimd.tensor_tensor(ob_t[:, hl, :], ob_t[:, hl, :], otmp[oi_][:], ALU.add),
                                     reads=[B(f"otmp{oi_}"), obn], writes=[obn])

                        for hl in range(8 if b_sub >= 1 else 0):
                            h = g * 8 + hl
                            sl = SLOPES[h]
                            oacc, on = next_ob()
                            nb = NBC[h]
                            for bo in range(nb):
                                bblk = 6 - bo
                                tix = (0 if bo == 0 else 1) * 16 + i

                                def tab(tmp, st, stn, tmpn, tix=tix, sl=sl):
                                    S.op("dve", lambda: nc.vector.scalar_tensor_tensor(tmp, cd[:, tix, :], -sl, st, ALU.mult, ALU.add),
                                         reads=[stn, B("cd")], writes=[tmpn])

                                unit(kcmpT[:, bblk * 128:(bblk + 1) * 128], qB[qi][:, hl, :], vcmp[:, bblk, :], oacc, on,
                                     bo == 0, bo == nb - 1, [B("kcmpT"), qn, B("vcmp")], tab,
                                     (0.0 if bo == 0 else -sl * 2048 * (bo - 1)),
                                     extra=((oacc[:, 132 + 32 * bblk:132 + 32 * bblk + 33], ov[:], [B("ov")]) if b_sub >= 2 else None))
                            if b_sub >= 3:
                                finish_head(oacc, on, hl, 0, True, imp_lo=(32 * (7 - nb) if b_sub >= 4 else None))
                        if b_stage < 3:
                            continue
                        S.op("dve", lambda: nc.vector.max(m8[:, 0:8], impa[:]), reads=[B("impa")], writes=[B("m8")])
                        S.op("dve", lambda: nc.vector.match_replace(work[:], m8[:, 0:8], impa[:], -1.0e30),
                             reads=[B("impa"), B("m8")], writes=[B("work")])
                        S.op("dve", lambda: nc.vector.max(m8[:, 8:16], work[:]), reads=[B("work")], writes=[B("m8")])
                        S.op("dve", lambda: nc.vector.tensor_scalar(sel[:], impa[:], m8[:, 15:16], None, ALU.is_ge),
                             reads=[B("impa"), B("m8")], writes=[B("sel")])
                        S.op("dve", lambda: nc.vector.scalar_tensor_tensor(sel[:], impa[:], -500.0, sel[:], ALU.is_gt, ALU.mult),
                             reads=[B("impa"), B("sel")], writes=[B("sel")])
                        S.op("dve", lambda: nc.vector.tensor_scalar(selb[:], sel[:], -1.0, None, ALU.add),
                             reads=[B("sel")], writes=[B("selb")])
                        for c2 in range(2):
                            S.op("pe", lambda: nc.tensor.transpose(selTp[:, c2, :], selb[:, c2 * 128:(c2 + 1) * 128], idb[:]),
                                 reads=[B("selb"), B("idb")], writes=[B("selTp")])
                        S.op("act", lambda: nc.scalar.copy(selT[:], selTp[:]), reads=[B("selTp")], writes=[B("selT")])
                        if b_stage < 4:
                            continue
                        for hl in range(8):
                            h = g * 8 + hl
                            sl = SLOPES[h]
                            oacc, on = next_ob()
                            nb = NBS[h]
                            for off in range(nb):
                                wbk = 96 + i - off
                                ty = 1 if off == 0 else 0

                                def tab(tmp, st, stn, tmpn, ty=ty, sl=sl):
                                    S.op("dve", lambda: nc.vector.scalar_tensor_tensor(tmp, dqk[:, ty, :], -sl, st, ALU.mult, ALU.add),
                                         reads=[stn, B("dqk")], writes=[tmpn])

                                ch = wbk // 64
                                unit(ksflat[:, wbk * 128:(wbk + 1) * 128], qB[qi][:, hl, :], vs[:, wbk, :], oacc, on,
                                     off == 0, off == nb - 1, [B("ksT"), qn, B("vs")], tab, -sl * 128 * off,
                                     mask=(wt[:, 128 * (wbk % 64):128 * (wbk % 64) + 128], selT[:, ch, :], [B("wt"), B("selT")]))
                            finish_head(oacc, on, hl, 1, False)
                        if b_stage < 5:
                            continue
                        for hl in range(8):
                            h = g * 8 + hl
                            sl = SLOPES[h]
                            oacc, on = next_ob()
                            for off in range(5):
                                wbk = 4 + i - off
                                ty = 1 if off == 0 else 2 if off == 4 else 0

                                def tab(tmp, st, stn, tmpn, ty=ty, sl=sl):
                                    S.op("dve", lambda: nc.vector.scalar_tensor_tensor(tmp, dqk[:, ty, :], -sl, st, ALU.mult, ALU.add),
                                         reads=[stn, B("dqk")], writes=[tmpn])

                                unit(kwT[:, wbk * 128:(wbk + 1) * 128], qB[qi][:, hl, :], vw[:, wbk, :], oacc, on,
                                     off == 0, off == 4, [B("kwT"), qn, B("vw")], tab, -sl * 128 * off)
                            finish_head(oacc, on, hl, 2, False)
                        S.dma("sp", O["ob"][i * 128:(i + 1) * 128, g * 1024:(g + 1) * 1024], ob_t[:].rearrange("q h d -> q (h d)"),
                              reads=[obn], writes=[], tag=obn)


def emit_p3(nc, S, x, oa, ob, z, goa, gob, wout, ident, xnew, pfx="p3"):
    B = lambda n: pfx + n
    TP = 1024
    NT = TP // 128
    KC = D // 128
    with contextlib.ExitStack() as es:
        sb = lambda name, shape, dt: es.enter_context(nc.sbuf_tensor(pfx + name, shape, dt))
        yT = sb("yT", [128, KC, TP], BF16)
        wb = [sb(f"wb{i}", [128, KC, 512], BF16) for i in range(2)]
        gO = sb("gO", [128, D], F32)
        ot = [sb(f"ot{i}", [128, 2048], F32) for i in range(2)]
        zt = [sb(f"zt{i}", [128, 2048], F32) for i in range(2)]
        yb = sb("yb", [128, 2048], BF16)
        idb = sb("idb", [128, 128], BF16)
        ss = sb("ss", [128, 2], F32)
        xr = [sb(f"xr{i}", [128, 512], F32) for i in range(3)]
        xo = [sb(f"xo{i}", [128, 512], F32) for i in range(3)]
        acc = [es.enter_context(nc.psum_tensor(pfx + f"acc{i}", [128, 512], F32)) for i in range(6)]
        tps = [es.enter_context(nc.psum_tensor(pfx + f"tp{i}", [128, 512], BF16)) for i in range(2)]
        S.dma("sp", idb[:], ident[:, :], reads=[], writes=[B("idb")], tag=B("idb"))
        S.dma("sp", gO[:, 0:2048], goa.partition_broadcast(128), reads=[], writes=[B("gO")], tag=B("gO"))
        S.dma("sp", gO[:, 2048:4096], gob.partition_broadcast(128), reads=[], writes=[B("gO")], tag=B("gO"))
        cnt = {"acc": 0, "tp": 0, "w": 0, "x": 0, "ld": 0}
        for p in range(TOK // TP):
            t0 = p * TP
            for tt in range(NT):
                tok0 = t0 + tt * 128
                for half, osrc in ((0, oa), (1, ob)):
                    li = cnt["ld"] % 2
                    cnt["ld"] += 1
                    o_t, z_t = ot[li], zt[li]
                    on, zn = B(f"ot{li}"), B(f"zt{li}")
                    S.dma("sp", o_t[:], osrc[tok0:tok0 + 128, :], reads=[], writes=[on], tag=on)
                    S.dma("sp", z_t[:], z[tok0:tok0 + 128, half * 2048:(half + 1) * 2048], reads=[], writes=[zn], tag=zn)
                    S.op("act", lambda: nc.scalar.activation(yb[:], o_t[:], AF.Square, accum_out=ss[:, 0:1]),
                         reads=[on], writes=[B("yb"), B("ss")])
                    S.op("dve", lambda: nc.vector.tensor_scalar(ss[:, 1:2], ss[:, 0:1], 1.0 / 2048, EPS, ALU.mult, ALU.add),
                         reads=[B("ss")], writes=[B("ss")])
                    S.op("act", lambda: nc.scalar.activation(ss[:, 1:2], ss[:, 1:2], AF.Sqrt), reads=[B("ss")], writes=[B("ss")])
                    S.op("dve", lambda: nc.vector.reciprocal(ss[:, 1:2], ss[:, 1:2]), reads=[B("ss")], writes=[B("ss")])
                    S.op("act", lambda: nc.scalar.activation(z_t[:], z_t[:], AF.Silu), reads=[zn], writes=[zn])
                    S.op("pool", lambda: nc.gpsimd.tensor_tensor(z_t[:], z_t[:], gO[:, half * 2048:(half + 1) * 2048], ALU.mult),
                         reads=[zn, B("gO")], writes=[zn])
                    S.op("dve", lambda: nc.vector.tensor_scalar_mul(o_t[:], o_t[:], ss[:, 1:2]), reads=[on, B("ss")], writes=[on])
                    S.op("dve", lambda: nc.vector.tensor_tensor(yb[:], o_t[:], z_t[:], ALU.mult), reads=[on, zn], writes=[B("yb")])
                    for k4 in range(4):
                        tpi = cnt["tp"] % 2
                        cnt["tp"] += 1
                        tp = tps[tpi]
                        for j in range(4):
                            kc = k4 * 4 + j
                            S.op("pe", lambda: nc.tensor.transpose(tp[:, j * 128:(j + 1) * 128], yb[:, kc * 128:(kc + 1) * 128], idb[:]),
                                 reads=[B("yb"), B("idb")], writes=[B(f"tp{tpi}")])
                        dst = yT[:, half * 16 + k4 * 4:half * 16 + k4 * 4 + 4, tt * 128:(tt + 1) * 128]
                        src = tp[:].rearrange("p (j t) -> p j t", t=128)
                        if k4 % 2 == 0:
                            S.op("dve", lambda: nc.vector.tensor_copy(dst, src), reads=[B(f"tp{tpi}")], writes=[B("yT")])
                        else:
                            S.op("act", lambda: nc.scalar.copy(dst, src), reads=[B(f"tp{tpi}")], writes=[B("yT")])
            for ch in range(8):
                wi = cnt["w"] % 2
                cnt["w"] += 1
                w, wn = wb[wi], B(f"wb{wi}")
                S.dma("pool", w[:], wout[:, ch * 512:(ch + 1) * 512].rearrange("(kc p) n -> p kc n", p=128),
                      reads=[], writes=[wn], tag=wn)
                for tt in range(NT):
                    tok0 = t0 + tt * 128
                    ai = cnt["acc"] % 6
                    cnt["acc"] += 1
                    a, an = acc[ai], B(f"acc{ai}")
                    xi = cnt["x"] % 3
                    cnt["x"] += 1
                    xn = B(f"xr{xi}")
                    S.dma("sp", xr[xi][:], x[tok0:tok0 + 128, ch * 512:(ch + 1) * 512], reads=[], writes=[xn], tag=xn)
                    for kc in range(KC):
                        S.op("pe", lambda: nc.tensor.matmul(a[:], yT[:, kc, tt * 128:(tt + 1) * 128], w[:, kc, :],
                                                            start=(kc == 0), stop=(kc == KC - 1)),
                             reads=[wn, B("yT")], writes=[an])
                    xon = B(f"xo{xi}")
                    S.op("dve", lambda: nc.vector.tensor_tensor(xo[xi][:], a[:], xr[xi][:], ALU.add), reads=[an, xn], writes=[xon])
                    S.dma("sp", xnew[tok0:tok0 + 128, ch * 512:(ch + 1) * 512], xo[xi][:], reads=[xon], writes=[], tag=xon)


def emit_p4(nc, S, xin, fg, out, pfx="p4"):
    B = lambda n: pfx + n
    with contextlib.ExitStack() as es:
        sb = lambda name, shape, dt: es.enter_context(nc.sbuf_tensor(pfx + name, shape, dt))
        gF = sb("gF", [128, D], F32)
        xt = [sb(f"xt{i}", [128, D], F32) for i in range(2)]
        junk = sb("junk", [128, D], BF16)
        ss = sb("ss", [128, 2], F32)
        S.dma("sp", gF[:], fg.partition_broadcast(128), reads=[], writes=[B("gF")], tag=B("gF"))
        for tt in range(TOK // 128):
            xi = tt % 2
            xn = B(f"xt{xi}")
            S.dma("sp", xt[xi][:], xin[tt * 128:(tt + 1) * 128, :], reads=[], writes=[xn], tag=xn)
            S.op("act", lambda: nc.scalar.activation(junk[:], xt[xi][:], AF.Square, accum_out=ss[:, 0:1]),
                 reads=[xn], writes=[B("junk"), B("ss")])
            S.op("dve", lambda: nc.vector.tensor_scalar(ss[:, 1:2], ss[:, 0:1], 1.0 / D, EPS, ALU.mult, ALU.add),
                 reads=[B("ss")], writes=[B("ss")])
            S.op("act", lambda: nc.scalar.activation(ss[:, 1:2], ss[:, 1:2], AF.Sqrt), reads=[B("ss")], writes=[B("ss")])
            S.op("dve", lambda: nc.vector.reciprocal(ss[:, 1:2], ss[:, 1:2]), reads=[B("ss")], writes=[B("ss")])
            S.op("dve", lambda: nc.vector.tensor_scalar_mul(xt[xi][:], xt[xi][:], ss[:, 1:2]), reads=[xn, B("ss")], writes=[xn])
            S.op("dve", lambda: nc.vector.tensor_tensor(xt[xi][:], xt[xi][:], gF[:], ALU.mult), reads=[xn, B("gF")], writes=[xn])
            S.dma("sp", out[tt * 128:(tt + 1) * 128, :], xt[xi][:], reads=[xn], writes=[], tag=xn)


P2_IN_SPECS = {
    "qaT": ([NHA, 128, TOK], BF16), "qbT": ([NHB, 128, TOK], BF16), "gl": ([TOK, 48], F32),
    "kaTw": ([2, NHA, 128, TOK], BF16), "vaw": ([2, TOK, NHA, 129], BF16),
    "kvTw": ([7, 4, 2, 128, TOK], BF16), "vsww": ([7, 2, TOK, 2, 129], BF16),
    "cw1": ([2, 4096, 256], F32), "cw2": ([2, 256, 128], F32), "cpe": ([2, 32, 128], F32),
}


def window(arr, core, nback):
    out = np.zeros((nback + 1,) + arr.shape[1:], arr.dtype)
    for j in range(nback + 1):
        r = core - nback + j
        if r >= 0:
            out[j] = arr[r]
    return out


def p2_inputs(core, own, G, cw, tables):
    m = {"qaT": own["qaT"], "qbT": own["qbT"], "gl": own["gl"],
         "kaTw": window(G["kaT"], core, 1), "vaw": window(G["va"], core, 1),
         "kvTw": window(G["kvT"], core, NPADR), "vsww": window(G["vsw"], core, NPADR)}
    m.update(cw)
    m.update(tables)
    return m


def build_p2_only(do_a=True, do_b=True, **kw):
    nc = bass.Bass("TRN2", target_bir_lowering=False)
    I = {k: nc.dram_tensor(k, shp, dt, kind="ExternalInput").ap() for k, (shp, dt) in {**P2_IN_SPECS, **TABLE_SPECS}.items()}
    O = {k: nc.dram_tensor(k, [TOK, 2048], F32, kind="ExternalOutput").ap() for k in ("oa", "ob")}
    S = Sched(nc)
    emit_p2(nc, S, I, O, do_a=do_a, do_b=do_b, **kw)
    S.drain()
    return nc


def _dram_in(nc, specs):
    return {k: nc.dram_tensor(k, shp, dt, kind="ExternalInput").ap() for k, (shp, dt) in specs.items()}


POST_IN_SPECS = {"x": ([TOK, D], F32), "z": ([TOK, D], F32), "goa": ([2048], F32), "gob": ([2048], F32),
                 "wout": ([D, D], F32)}


def build_first():
    return build_p1_only()


def build_mid():
    nc = bass.Bass("TRN2", target_bir_lowering=False)
    I = _dram_in(nc, {**P2_IN_SPECS, **TABLE_SPECS, **POST_IN_SPECS, "ng": ([D], F32), "win": ([D, DIN], F32)})
    o = {k: nc.dram_tensor("n_" + k, shp, dt, kind="ExternalOutput").ap() for k, (shp, dt) in p1_out_specs().items()}
    xnew = nc.dram_tensor("xnew", [TOK, D], F32, kind="ExternalOutput").ap()
    O = {k: nc.dram_tensor("s_" + k, [TOK, 2048], F32).ap() for k in ("oa", "ob")}
    S = Sched(nc)
    emit_p2(nc, S, I, O)
    S.drain()
    emit_p3(nc, S, I["x"], O["oa"], O["ob"], I["z"], I["goa"], I["gob"], I["wout"], I["ident"], xnew)
    S.drain()
    emit_p1(nc, S, xnew, I["ng"], I["win"], I["ident"], o)
    S.drain()
    return nc


def build_last():
    nc = bass.Bass("TRN2", target_bir_lowering=False)
    I = _dram_in(nc, {**P2_IN_SPECS, **TABLE_SPECS, **POST_IN_SPECS, "fg": ([D], F32)})
    out = nc.dram_tensor("out", [TOK, D], F32, kind="ExternalOutput").ap()
    xnew = nc.dram_tensor("s_xnew", [TOK, D], F32).ap()
    O = {k: nc.dram_tensor("s_" + k, [TOK, 2048], F32).ap() for k in ("oa", "ob")}
    S = Sched(nc)
    emit_p2(nc, S, I, O)
    S.drain()
    emit_p3(nc, S, I["x"], O["oa"], O["ob"], I["z"], I["goa"], I["gob"], I["wout"], I["ident"], xnew)
    S.drain()
    emit_p4(nc, S, xnew, I["fg"], out)
    S.drain()
    return nc


def build_p3_test():
    nc = bass.Bass("TRN2", target_bir_lowering=False)
    I = _dram_in(nc, {**POST_IN_SPECS, "oa": ([TOK, 2048], F32), "ob": ([TOK, 2048], F32), "fg": ([D], F32),
                      "ident": ([128, 128], BF16)})
    out = nc.dram_tensor("out", [TOK, D], F32, kind="ExternalOutput").ap()
    xnew = nc.dram_tensor("xnew", [TOK, D], F32, kind="ExternalOutput").ap()
    S = Sched(nc)
    emit_p3(nc, S, I["x"], I["oa"], I["ob"], I["z"], I["goa"], I["gob"], I["wout"], I["ident"], xnew)
    S.drain()
    emit_p4(nc, S, xnew, I["fg"], out)
    S.drain()
    return nc


def _run(nc, in_maps):
    res = run_bass_kernel_spmd(nc, in_maps, core_ids=list(range(NCORES)))
    return res.results


def kernel(x, norm_g, w_in, cmp_k_pe, cmp_k_w1, cmp_k_w2, cmp_v_pe, cmp_v_w1, cmp_v_w2,
           out_g_a, out_g_b, w_out, final_g):
    f32 = lambda a: np.ascontiguousarray(np.asarray(a, dtype=np.float32))
    x2 = f32(x).reshape(SEQ, D)
    norm_g, w_in, w_out, final_g = f32(norm_g), f32(w_in), f32(w_out), f32(final_g)
    out_g_a, out_g_b = f32(out_g_a), f32(out_g_b)
    cw1 = np.stack([f32(cmp_k_w1), f32(cmp_v_w1)], 1)
    cw2 = np.stack([f32(cmp_k_w2), f32(cmp_v_w2)], 1)
    cpe = np.stack([f32(cmp_k_pe), f32(cmp_v_pe)], 1)
    tables = [make_tables(c) for c in range(NCORES)]
    ident = tables[0]["ident"]
    xcur = [x2[c * TOK:(c + 1) * TOK] for c in range(NCORES)]
    nc = build_first()
    res = _run(nc, [{"x": xcur[c], "ng": norm_g[0], "win": w_in[0], "ident": ident} for c in range(NCORES)])
    nc_mid = None
    out = None
    for l in range(DEPTH):
        G = {k: np.stack([np.asarray(res[c][k]) for c in range(NCORES)]) for k in ("kaT", "va", "kvT", "vsw")}
        cw = {"cw1": cw1[l], "cw2": cw2[l], "cpe": cpe[l]}
        in_maps = []
        for c in range(NCORES):
            m = p2_inputs(c, res[c], G, cw, tables[c])
            m.update(x=xcur[c], z=np.asarray(res[c]["z"]), goa=out_g_a[l], gob=out_g_b[l], wout=w_out[l])
            if l < DEPTH - 1:
                m.update(ng=norm_g[l + 1], win=w_in[l + 1])
            else:
                m.update(fg=final_g)
            in_maps.append(m)
        if l < DEPTH - 1:
            if nc_mid is None:
                nc_mid = build_mid()
            r = _run(nc_mid, in_maps)
            xcur = [np.asarray(r[c]["xnew"]) for c in range(NCORES)]
            res = [{k: np.asarray(r[c]["n_" + k]) for k in p1_out_specs()} for c in range(NCORES)]
        else:
            r = _run(build_last(), in_maps)
            out = np.concatenate([np.asarray(r[c]["out"]) for c in range(NCORES)], 0)
    return out.reshape(1, SEQ, D).astype(np.float32)
```

```python
import contextlib
import functools
import numpy as np
import ml_dtypes
import concourse.bass as bass
import concourse.mybir as mybir
from concourse.bass_utils import run_bass_kernel_spmd

F32 = mybir.dt.float32
BF16 = mybir.dt.bfloat16
AF = mybir.ActivationFunctionType
ALU = mybir.AluOpType
AX = mybir.AxisListType

NCORES = 8
D = 4096
SEQ = 16384
TOK = SEQ // NCORES
HD = 128
NHA = 16
NHB = 16
DIN = 13872
DEPTH = 4
EPS = 1e-6
QSCALE = HD ** -0.5
C_QA, C_KA, C_VA, C_ZA, C_QB = 0, 2048, 4096, 6144, 8192
C_KC, C_VC, C_KS, C_VS, C_KW, C_VW, C_ZB, C_G = 10240, 10496, 10752, 11008, 11264, 11520, 11776, 13824


class Sched:
    def __init__(self, nc):
        self.nc = nc
        self.eng = {"pe": nc.tensor, "dve": nc.vector, "act": nc.scalar, "pool": nc.gpsimd, "sp": nc.sync}
        self.esem, self.ecnt = {}, {}
        self.seen = {k: {} for k in self.eng}
        self.bufs = {}
        self.dsem, self.dcnt = {}, {}
        self._ctx = contextlib.ExitStack()
        for k in self.eng:
            self.esem[k] = self._ctx.enter_context(nc.semaphore("s_" + k))
            self.ecnt[k] = 0

    def _buf(self, b):
        st = self.bufs.get(b)
        if st is None:
            st = self.bufs[b] = [{}, {}]
        return st

    def _deps(self, reads, writes):
        d = {}

        def add(m):
            for k, sv in m.items():
                if k not in d or d[k][1] < sv[1]:
                    d[k] = sv

        for b in reads:
            add(self._buf(b)[0])
        for b in writes:
            w, r = self._buf(b)
            add(w)
            add(r)
        return d

    def _wait(self, e, deps, skip_self=False):
        seen = self.seen[e]
        for k, (s, v) in deps.items():
            if skip_self and k == "E" + e:
                continue
            if seen.get(k, -1) >= v:
                continue
            self.eng[e].wait_ge(s, v)
            seen[k] = v

    def _commit(self, reads, writes, key, sem, val):
        for b in reads:
            r = self._buf(b)[1]
            if key not in r or r[key][1] < val:
                r[key] = (sem, val)
        for b in writes:
            st = self._buf(b)
            st[0] = {key: (sem, val)}
            st[1] = {}

    def op(self, e, fn, reads=(), writes=()):
        deps = self._deps(reads, writes)
        self._wait(e, deps, skip_self=(e == "pe"))
        ins = fn()
        self.ecnt[e] += 1
        ins.then_inc(self.esem[e], 1)
        self._commit(reads, writes, "E" + e, self.esem[e], self.ecnt[e])
        return ins

    def dma(self, q, out, in_, reads, writes, tag, **kw):
        if tag not in self.dsem:
            self.dsem[tag] = self._ctx.enter_context(self.nc.semaphore("d_" + str(tag)))
            self.dcnt[tag] = 0
        deps = self._deps(reads, writes)
        self._wait(q, deps)
        ins = self.eng[q].dma_start(out=out, in_=in_, **kw)
        self.dcnt[tag] += 16
        ins.then_inc(self.dsem[tag], 16)
        self._commit(reads, writes, "D" + str(tag), self.dsem[tag], self.dcnt[tag])
        return ins

    def finish(self, e, bufs):
        self._wait(e, self._deps(bufs, bufs))

    def drain(self, engines=None):
        deps = {}
        for k in self.eng:
            if self.ecnt[k] > 0:
                deps["E" + k] = (self.esem[k], self.ecnt[k])
        for t, s_ in self.dsem.items():
            if self.dcnt[t] > 0:
                deps["D" + str(t)] = (s_, self.dcnt[t])
        for e in (engines or self.eng):
            self._wait(e, deps)


def emit_p1(nc, S, x, ng, win, ident, o, pfx="p1"):
    TP = 1024
    NT = TP // 128
    KC = D // 128
    with contextlib.ExitStack() as es:
        sb = lambda name, shape, dt: es.enter_context(nc.sbuf_tensor(pfx + name, shape, dt))
        hT = sb("hT", [128, KC, TP], BF16)
        wb = [sb(f"wb{i}", [128, KC, 512], BF16) for i in range(2)]
        xt = [sb(f"xt{i}", [128, D], F32) for i in range(2)]
        xs = sb("xs", [128, D], BF16)
        gT = sb("gT", [128, KC], F32)
        idb = sb("idb", [128, 128], BF16)
        ss = sb("ss", [128, 2], F32)
        stF = [sb(f"stF{i}", [128, 512], BF16) for i in range(3)]
        stV = [sb(f"stV{i}", [128, 4, 129], BF16) for i in range(3)]
        stZ = [sb(f"stZ{i}", [128, 512], F32) for i in range(3)]
        acc = [es.enter_context(nc.psum_tensor(pfx + f"acc{i}", [128, 512], F32)) for i in range(6)]
        tps = [es.enter_context(nc.psum_tensor(pfx + f"tp{i}", [128, 512], BF16)) for i in range(2)]
        B = lambda n: pfx + n

        nc_ = nc
        S.dma("sp", idb[:], ident[:, :], reads=[], writes=[B("idb")], tag=B("idb"))
        with nc.allow_non_contiguous_dma(reason="tiny gain vector"):
            S.dma("sp", gT[:], ng.rearrange("(c p) -> p c", p=128), reads=[], writes=[B("gT")], tag=B("gT"))
        for i in range(3):
            S.op("dve", lambda: nc.vector.memset(stV[i][:], 1.0), writes=[B(f"stV{i}")])

        cnt = {"acc": 0, "tp": 0, "F": 0, "V": 0, "Z": 0, "w": 0, "ev": 0}

        def evac(dst, src, reads, writes, scale=None):
            cnt["ev"] += 1
            if cnt["ev"] % 2 == 0:
                if scale is None:
                    S.op("dve", lambda: nc.vector.tensor_copy(dst, src), reads=reads, writes=writes)
                else:
                    S.op("dve", lambda: nc.vector.tensor_scalar_mul(dst, src, scale), reads=reads, writes=writes)
            else:
                S.op("act", lambda: nc.scalar.activation(dst, src, AF.Copy, scale=(1.0 if scale is None else scale)),
                     reads=reads, writes=writes)

        for p in range(TOK // TP):
            t0 = p * TP
            for tt in range(NT):
                xb = xt[tt % 2]
                xbn = B(f"xt{tt%2}")
                S.dma("sp", xb[:], x[t0 + tt * 128:t0 + (tt + 1) * 128, :], reads=[], writes=[xbn], tag=xbn)
                S.op("act", lambda: nc.scalar.activation(xs[:], xb[:], AF.Square, accum_out=ss[:, 0:1]),
                     reads=[xbn], writes=[B("xs"), B("ss")])
                S.op("dve", lambda: nc.vector.tensor_scalar(ss[:, 1:2], ss[:, 0:1], 1.0 / D, EPS, ALU.mult, ALU.add),
                     reads=[B("ss")], writes=[B("ss")])
                S.op("act", lambda: nc.scalar.activation(ss[:, 1:2], ss[:, 1:2], AF.Sqrt),
                     reads=[B("ss")], writes=[B("ss")])
                S.op("dve", lambda: nc.vector.reciprocal(ss[:, 1:2], ss[:, 1:2]),
                     reads=[B("ss")], writes=[B("ss")])
                S.op("act", lambda: nc.scalar.activation(xs[:], xb[:], AF.Copy, scale=ss[:, 1:2]),
                     reads=[xbn, B("ss")], writes=[B("xs")])
                for k4 in range(KC // 4):
                    tpi = cnt["tp"] % 2
                    cnt["tp"] += 1
                    tp = tps[tpi]
                    for j in range(4):
                        kc = k4 * 4 + j
                        S.op("pe", lambda: nc.tensor.transpose(tp[:, j * 128:(j + 1) * 128], xs[:, kc * 128:(kc + 1) * 128], idb[:]),
                             reads=[B("xs"), B("idb")], writes=[B(f"tp{tpi}")])
                    for j in range(4):
                        kc = k4 * 4 + j
                        dst = hT[:, kc, tt * 128:(tt + 1) * 128]
                        src = tp[:, j * 128:(j + 1) * 128]
                        if j % 2 == 0:
                            S.op("dve", lambda: nc.vector.tensor_scalar_mul(dst, src, gT[:, kc:kc + 1]),
                                 reads=[B(f"tp{tpi}"), B("gT")], writes=[B("hT")])
                        else:
                            S.op("act", lambda: nc.scalar.activation(dst, src, AF.Copy, scale=gT[:, kc:kc + 1]),
                                 reads=[B(f"tp{tpi}"), B("gT")], writes=[B("hT")])

            def load_w(c0, ncols):
                wi = cnt["w"] % 2
                cnt["w"] += 1
                S.dma("pool", wb[wi][:, :, 0:ncols], win[:, c0:c0 + ncols].rearrange("(kc p) n -> p kc n", p=128),
                      reads=[], writes=[B(f"wb{wi}")], tag=B(f"wb{wi}"))
                return wb[wi], B(f"wb{wi}")

            def next_acc():
                ai = cnt["acc"] % 6
                cnt["acc"] += 1
                return acc[ai], B(f"acc{ai}")

            def feat_tile(w, wn, wc0, dst_fn, scale):
                for half in range(TP // 512):
                    a, an = next_acc()
                    for kc in range(KC):
                        S.op("pe", lambda: nc.tensor.matmul(a[:], w[:, kc, wc0:wc0 + 128], hT[:, kc, half * 512:(half + 1) * 512],
                                                            start=(kc == 0), stop=(kc == KC - 1)),
                             reads=[wn, B("hT")], writes=[an])
                    si = cnt["F"] % 3
                    cnt["F"] += 1
                    evac(stF[si][:], a[:], [an], [B(f"stF{si}")], scale)
                    S.dma("sp", dst_fn(t0 + half * 512, 512), stF[si][:], reads=[B(f"stF{si}")], writes=[], tag=B(f"stF{si}"))

            def tok_tiles(w, wn, wc0, ncols, kind, dst_fn):
                for tt in range(NT):
                    a, an = next_acc()
                    for kc in range(KC):
                        S.op("pe", lambda: nc.tensor.matmul(a[:, 0:ncols], hT[:, kc, tt * 128:(tt + 1) * 128], w[:, kc, wc0:wc0 + ncols],
                                                            start=(kc == 0), stop=(kc == KC - 1)),
                             reads=[wn, B("hT")], writes=[an])
                    tok0 = t0 + tt * 128
                    if kind == "V":
                        si = cnt["V"] % 3
                        cnt["V"] += 1
                        nh = ncols // 128
                        evac(stV[si][:, 0:nh, 0:128], a[:, 0:ncols].rearrange("p (h d) -> p h d", d=128),
                             [an], [B(f"stV{si}")])
                        S.dma("sp", dst_fn(tok0), stV[si][:, 0:nh, :], reads=[B(f"stV{si}")], writes=[], tag=B(f"stV{si}"))
                    else:
                        si = cnt["Z"] % 3
                        cnt["Z"] += 1
                        evac(stZ[si][:, 0:ncols], a[:, 0:ncols], [an], [B(f"stZ{si}")])
                        S.dma("sp", dst_fn(tok0), stZ[si][:, 0:ncols], reads=[B(f"stZ{si}")], writes=[], tag=B(f"stZ{si}"))

            for (c0, dst, scale) in ((C_QA, o["qaT"], QSCALE), (C_KA, o["kaT"], None), (C_QB, o["qbT"], QSCALE)):
                for ch in range(4):
                    w, wn = load_w(c0 + ch * 512, 512)
                    for sub in range(4):
                        h = ch * 4 + sub
                        feat_tile(w, wn, sub * 128, lambda tk, n, h=h, dst=dst: dst[h, :, tk:tk + n], scale)
            w, wn = load_w(C_KC, 512)
            for sub in range(4):
                feat_tile(w, wn, sub * 128, lambda tk, n, sub=sub: o["kvT"][sub // 2, sub % 2, :, tk:tk + n], None)
            w, wn = load_w(C_KS, 512)
            for sub in range(2):
                feat_tile(w, wn, sub * 128, lambda tk, n, sub=sub: o["kvT"][2, sub, :, tk:tk + n], None)
            tok_tiles(w, wn, 256, 256, "V", lambda tok0: o["vsw"][0, tok0:tok0 + 128, :, :])
            w, wn = load_w(C_KW, 512)
            for sub in range(2):
                feat_tile(w, wn, sub * 128, lambda tk, n, sub=sub: o["kvT"][3, sub, :, tk:tk + n], None)
            tok_tiles(w, wn, 256, 256, "V", lambda tok0: o["vsw"][1, tok0:tok0 + 128, :, :])
            for ch in range(4):
                w, wn = load_w(C_VA + ch * 512, 512)
                tok_tiles(w, wn, 0, 512, "V", lambda tok0, ch=ch: o["va"][tok0:tok0 + 128, ch * 4:(ch + 1) * 4, :])
            for (c0, zoff) in ((C_ZA, 0), (C_ZB, 2048)):
                for ch in range(4):
                    w, wn = load_w(c0 + ch * 512, 512)
                    tok_tiles(w, wn, 0, 512, "Z",
                              lambda tok0, ch=ch, zoff=zoff: o["z"][tok0:tok0 + 128, zoff + ch * 512:zoff + (ch + 1) * 512])
            w, wn = load_w(C_G, 48)
            tok_tiles(w, wn, 0, 48, "Z", lambda tok0: o["gl"][tok0:tok0 + 128, :])

        outbufs = [B(f"stF{i}") for i in range(3)] + [B(f"stV{i}") for i in range(3)] + [B(f"stZ{i}") for i in range(3)]
        return outbufs


def p1_out_specs():
    return {
        "qaT": ([NHA, 128, TOK], BF16), "kaT": ([NHA, 128, TOK], BF16), "va": ([TOK, NHA, 129], BF16),
        "qbT": ([NHB, 128, TOK], BF16), "kvT": ([4, 2, 128, TOK], BF16), "vsw": ([2, TOK, 2, 129], BF16),
        "z": ([TOK, D], F32), "gl": ([TOK, 48], F32),
    }


def build_p1_only():
    nc = bass.Bass("TRN2", target_bir_lowering=False)
    x = nc.dram_tensor("x", [TOK, D], F32, kind="ExternalInput").ap()
    ng = nc.dram_tensor("ng", [D], F32, kind="ExternalInput").ap()
    win = nc.dram_tensor("win", [D, DIN], F32, kind="ExternalInput").ap()
    ident = nc.dram_tensor("ident", [128, 128], BF16, kind="ExternalInput").ap()
    o = {k: nc.dram_tensor(k, shp, dt, kind="ExternalOutput").ap() for k, (shp, dt) in p1_out_specs().items()}
    S = Sched(nc)
    bufs = emit_p1(nc, S, x, ng, win, ident, o)
    S.finish("sp", bufs)
    return nc


SLOPES = [2.0 ** (-(k + 1) / 2.0) for k in range(16)]
NEGBIG = -30000.0
POSBIG = 1.0e9
NEED = [40.0 / s for s in SLOPES]
NBS = [min(int(np.ceil(n / 128.0)) + 2, 97) for n in NEED]
NBC = [min(int((n + 16 + 2063) // 2048) + 1, 7) for n in NEED]
NPADR = 6
WTOK = (NPADR + 1) * TOK
NCMP = WTOK // 16
NJW = WTOK // 64


def a_type(off):
    return 0 if off == 0 else 1 if off == 1 else 2 if off in (2, 3) else 3 if off == 4 else 4 if off < 16 else 5


def make_tables(core):
    k = np.arange(128)[:, None].astype(np.float64)
    q = np.arange(128)[None, :].astype(np.float64)
    dqk = q - k
    t = {}
    dq = np.zeros((3, 128, 128), np.float32)
    dq[0] = dqk
    dq[1] = np.where(dqk >= 0, dqk, POSBIG)
    dq[2] = np.where(dqk < 0, dqk, POSBIG)
    t["dqk"] = dq
    lm = np.zeros((6, 128, 128), np.float32)
    for ty, off in enumerate((0, 1, 2, 4, 5, 16)):
        dl = 128 * off + dqk
        mult = ((dl >= 0) & (dl <= 128)).astype(np.float64)
        mult += ((dl >= 0) & (dl % 4 == 0) & (dl <= 512))
        mult += ((dl >= 0) & (dl % 16 == 0) & (dl <= 2048))
        if ty == 2:
            dl3 = 128 * 3 + dqk
            m3 = ((dl3 % 4 == 0) & (dl3 <= 512)).astype(np.float64) + ((dl3 % 16 == 0) & (dl3 <= 2048))
            assert np.array_equal(m3, mult)
        lm[ty] = np.where(mult > 0, np.log(np.maximum(mult, 1)), NEGBIG)
    t["lm"] = lm
    cd = np.zeros((2, 16, 128, 128), np.float32)
    for i in range(16):
        d0 = q - 16 * k + 128 * i - 31
        cd[0, i] = np.where(d0 >= 0, d0, POSBIG)
        d1 = d0 + 2048
        cd[1, i] = np.where(d1 >= 0, d1, POSBIG)
    t["cd"] = cd
    ov = np.zeros((128, 33), np.float32)
    for m in range(33):
        ov[:, m] = ((k[:, 0] >= 4 * m - 1) & (k[:, 0] <= 4 * m + 3))
    t["ov"] = ov.astype(ml_dtypes.bfloat16)
    wt = np.zeros((128, 8192), np.float32)
    u = np.arange(8192)
    wt[u // 64, u] = -NEGBIG
    t["wt"] = wt.astype(ml_dtypes.bfloat16)
    fb = np.zeros((16, 128, 256), np.float32)
    jw = np.arange(256)[None, :]
    for i in range(16):
        curw = 192 + 2 * i + (np.arange(128)[:, None] >= 64)
        jabs = jw - 192 + 32 * core
        valid = (jabs >= 0) & (jw <= curw) & (jw < NJW)
        forced = valid & ((jabs == 0) | ((curw - jw) < 2))
        fb[i] = np.where(forced, 1000.0 + jw, np.where(valid, 0.0, -1000.0 - jw))
    t["fb"] = fb
    a = np.arange(NCMP)
    cval = ((a - NPADR * 128 + 128 * core) >= 0).astype(np.float32)
    t["cval"] = np.ascontiguousarray(cval.reshape(7, 128).T)
    t["ident"] = np.eye(128, dtype=np.float32).astype(ml_dtypes.bfloat16)
    return t


TABLE_SPECS = {"dqk": ([3, 128, 128], F32), "lm": ([6, 128, 128], F32), "cd": ([2, 16, 128, 128], F32),
               "ov": ([128, 33], BF16), "wt": ([128, 8192], BF16), "fb": ([16, 128, 256], F32),
               "cval": ([128, 7], F32), "ident": ([128, 128], BF16)}


def emit_p2(nc, S, I, O, pfx="p2", do_a=True, do_b=True, b_groups=(0, 1), b_qblocks=tuple(range(16)), b_stage=9, b_sub=9):
    B = lambda n: pfx + n
    with contextlib.ExitStack() as es:
        sb = lambda name, shape, dt: es.enter_context(nc.sbuf_tensor(pfx + name, shape, dt))
        ps = lambda name, shape, dt: es.enter_context(nc.psum_tensor(pfx + name, shape, dt))
        dqk = sb("dqk", [128, 3, 128], F32)
        idb = sb("idb", [128, 128], BF16)
        NTP = 8
        LAG = 4
        tmpt = [sb(f"tmp{i}", [128, 128], F32) for i in range(NTP)]
        ptt = [sb(f"pt{i}", [128, 128], BF16) for i in range(NTP)]
        pend = []

        def flush():
            while pend:
                pend.pop(0)[1]()
        rd = [sb(f"rd{i}", [128, 4], F32) for i in range(2)]
        obank = [ps(f"ob{i}", [128, 512], F32) for i in range(2)]
        NST = 5
        stb = [ps(f"st{i}", [128, 512], F32) for i in range(NST)]
        xbank = obank
        selTp = ps("selTp", [128, 2, 128], BF16)
        with nc.allow_non_contiguous_dma(reason="small tables"):
            S.dma("sp", dqk[:], I["dqk"].rearrange("t k q -> k t q"), reads=[], writes=[B("dqk")], tag=B("dqk"))
        S.dma("sp", idb[:], I["ident"][:, :], reads=[], writes=[B("idb")], tag=B("idb"))
        cnt = {"st": 0, "tp": 0, "ob": 0, "rd": 0, "otmp": 0, "osta": 0}

        def unit(kT_ap, q_ap, v_ap, oacc, oname, first, last, rd_list, tab_fn, cbias, ncols=129, mask=None, extra=None):
            si = cnt["st"] % NST
            cnt["st"] += 1
            st = stb[si][:, 0:128]
            stn = B(f"st{si}")
            S.op("pe", lambda: nc.tensor.matmul(st, kT_ap, q_ap, start=True, stop=(mask is None)),
                 reads=rd_list, writes=[stn])
            if mask is not None:
                S.op("pe", lambda: nc.tensor.matmul(st, mask[0], mask[1], start=False, stop=True),
                     reads=mask[2], writes=[stn])
            ti = cnt["tp"] % NTP
            cnt["tp"] += 1
            tab_fn(tmpt[ti][:], st, stn, B(f"tmp{ti}"))
            S.op("act", lambda: nc.scalar.activation(ptt[ti][:], tmpt[ti][:], AF.Exp, bias=float(cbias), scale=1.0),
                 reads=[B(f"tmp{ti}")], writes=[B(f"pt{ti}")])

            def back(ti=ti, oacc=oacc, oname=oname, first=first, last=last, v_ap=v_ap, extra=extra, ncols=ncols, rd_list=rd_list):
                S.op("pe", lambda: nc.tensor.matmul(oacc[:, 0:ncols], ptt[ti][:], v_ap, start=first, stop=last, skip_group_check=(extra is not None)),
                     reads=[B(f"pt{ti}")] + rd_list, writes=[oname])
                if extra is not None:
                    S.op("pe", lambda: nc.tensor.matmul(extra[0], ptt[ti][:], extra[1], start=False, stop=last, skip_group_check=True),
                         reads=[B(f"pt{ti}")] + extra[2], writes=[oname])

            pend.append(("u", back))
            while sum(1 for k, _ in pend if k == "u") > LAG:
                pend.pop(0)[1]()

        def next_ob():
            oi = cnt["ob"] % 2
            cnt["ob"] += 1
            return obank[oi], B(f"ob{oi}")

        if do_a:
            with contextlib.ExitStack() as ea:
                sa = lambda name, shape, dt: ea.enter_context(nc.sbuf_tensor(pfx + name, shape, dt))
                lm = sa("lm", [128, 6, 128], F32)
                kTa = [sa(f"kTa{i}", [128, 2, TOK], BF16) for i in range(2)]
                vA = [sa(f"vA{i}", [128, 32, 129], BF16) for i in range(2)]
                qA = [sa(f"qA{i}", [128, TOK], BF16) for i in range(2)]
                ta = [sa(f"ta{i}", [128, 6, 128], F32) for i in range(2)]
                ost = [sa(f"osta{i}", [128, 128], F32) for i in range(3)]
                with nc.allow_non_contiguous_dma(reason="small tables"):
                    S.dma("sp", lm[:], I["lm"].rearrange("t k q -> k t q"), reads=[], writes=[B("lm")], tag=B("lm"))
                no = 0
                for h in range(NHA):
                    hb = h % 2
                    sl = SLOPES[h]
                    S.dma("sp", kTa[hb][:], I["kaTw"][:, h, :, :].rearrange("r d t -> d r t"),
                          reads=[], writes=[B(f"kTa{hb}")], tag=B(f"kTa{hb}"))
                    for r in range(2):
                        S.dma("sp", vA[hb][:, r * 16:(r + 1) * 16, :], I["vaw"][r, :, h, :].rearrange("(b p) e -> p b e", p=128),
                              reads=[], writes=[B(f"vA{hb}")], tag=B(f"vA{hb}"))
                    S.dma("sp", qA[hb][:], I["qaT"][h, :, :], reads=[], writes=[B(f"qA{hb}")], tag=B(f"qA{hb}"))
                    for ty in range(6):
                        S.op("dve", lambda: nc.vector.scalar_tensor_tensor(ta[hb][:, ty, :], dqk[:, 0, :], -sl, lm[:, ty, :], ALU.mult, ALU.add),
                             reads=[B("dqk"), B("lm")], writes=[B(f"ta{hb}")])
                    kflat = kTa[hb][:].rearrange("d r t -> d (r t)")
                    for i in range(16):
                        oacc, on = next_ob()
                        for off in range(17):
                            wbk = 16 + i - off
                            ty = a_type(off)

                            def tab(tmp, st, stn, tmpn, ty=ty):
                                S.op("dve", lambda: nc.vector.tensor_tensor(tmp, st, ta[hb][:, ty, :], ALU.add),
                                     reads=[stn, B(f"ta{hb}")], writes=[tmpn])

                            unit(kflat[:, wbk * 128:(wbk + 1) * 128], qA[hb][:, i * 128:(i + 1) * 128], vA[hb][:, wbk, :],
                                 oacc, on, off == 0, off == 16, [B(f"kTa{hb}"), B(f"qA{hb}"), B(f"vA{hb}")], tab, -sl * 128 * off)
                        def a_fin(oacc=oacc, on=on, h=h, i=i):
                            ri = cnt["rd"] % 2
                            cnt["rd"] += 1
                            S.op("dve", lambda: nc.vector.reciprocal(rd[ri][:, 0:1], oacc[:, 128:129]), reads=[on], writes=[B(f"rd{ri}")])
                            oi = cnt["osta"] % 3
                            cnt["osta"] += 1
                            S.op("act", lambda: nc.scalar.activation(ost[oi][:], oacc[:, 0:128], AF.Copy, scale=rd[ri][:, 0:1]),
                                 reads=[on, B(f"rd{ri}")], writes=[B(f"osta{oi}")])
                            S.dma("sp", O["oa"][i * 128:(i + 1) * 128, h * 128:(h + 1) * 128], ost[oi][:],
                                  reads=[B(f"osta{oi}")], writes=[], tag=B(f"osta{oi}"))

                        pend.append(("f", a_fin))
                flush()

        S.drain()
        if do_b:
            with contextlib.ExitStack() as eb:
                sB = lambda name, shape, dt: eb.enter_context(nc.sbuf_tensor(pfx + name, shape, dt))
                cd = sB("cd", [128, 32, 128], F32)
                fb = sB("fb", [128, 16, 256], F32)
                ov = sB("ov", [128, 33], BF16)
                wt = sB("wt", [128, 8192], BF16)
                cval = sB("cval", [128, 7], F32)
                ksT = sB("ksT", [128, 7, TOK], BF16)
                vs = sB("vs", [128, 112, 129], BF16)
                kwT = sB("kwT", [128, 20 * 128], BF16)
                vw = sB("vw", [128, 20, 129], BF16)
                kcmpT = sB("kcmpT", [128, NCMP], BF16)
                vcmp = sB("vcmp", [128, 7, 129], BF16)
                qB = [sB(f"qB{i}", [128, 8, 128], BF16) for i in range(2)]
                glt = [sB(f"gl{i}", [128, 48], F32) for i in range(2)]
                gs = sB("gs", [128, 48], F32)
                impa = sB("impa", [128, 256], F32)
                work = sB("work", [128, 256], F32)
                m8 = sB("m8", [128, 16], F32)
                sel = sB("sel", [128, 256], F32)
                selb = sB("selb", [128, 256], BF16)
                selT = sB("selT", [128, 2, 128], BF16)
                obst = [sB(f"obst{i}", [128, 8, 128], F32) for i in range(2)]
                otmp = [sB(f"otmp{i}", [128, 128], F32) for i in range(2)]
                with nc.allow_non_contiguous_dma(reason="small tables"):
                    S.dma("sp", cd[:], I["cd"].rearrange("a i k q -> k (a i) q"), reads=[], writes=[B("cd")], tag=B("cd"))
                    S.dma("sp", fb[:], I["fb"].rearrange("i q j -> q i j"), reads=[], writes=[B("fb")], tag=B("fb"))
                S.dma("sp", ov[:], I["ov"][:, :], reads=[], writes=[B("ov")], tag=B("ov"))
                S.dma("sp", wt[:], I["wt"][:, :], reads=[], writes=[B("wt")], tag=B("wt"))
                S.dma("sp", cval[:], I["cval"][:, :], reads=[], writes=[B("cval")], tag=B("cval"))
                ncoef = 0
                nobst = 0
                for g in b_groups:
                    S.dma("sp", ksT[:], I["kvTw"][:, 2, g, :, :].rearrange("r d t -> d r t"), reads=[], writes=[B("ksT")], tag=B("ksT"))
                    for r in range(7):
                        S.dma("sp", vs[:, r * 16:(r + 1) * 16, :], I["vsww"][r, 0, :, g, :].rearrange("(b p) e -> p b e", p=128),
                              reads=[], writes=[B("vs")], tag=B("vs"))
                    S.dma("sp", kwT[:, 0:512], I["kvTw"][5, 3, g, :, TOK - 512:TOK], reads=[], writes=[B("kwT")], tag=B("kwT"))
                    S.dma("sp", kwT[:, 512:], I["kvTw"][6, 3, g, :, :], reads=[], writes=[B("kwT")], tag=B("kwT"))
                    S.dma("sp", vw[:, 0:4, :], I["vsww"][5, 1, TOK - 512:TOK, g, :].rearrange("(b p) e -> p b e", p=128),
                          reads=[], writes=[B("vw")], tag=B("vw"))
                    S.dma("sp", vw[:, 4:20, :], I["vsww"][6, 1, :, g, :].rearrange("(b p) e -> p b e", p=128),
                          reads=[], writes=[B("vw")], tag=B("vw"))
                    if b_stage < 1:
                        continue
                    with contextlib.ExitStack() as ec:
                        sc = lambda name, shape, dt: ec.enter_context(nc.sbuf_tensor(pfx + name + str(g), shape, dt))
                        tT = sc("tT", [128, 7, TOK], BF16)
                        w1 = sc("w1", [128, 32, 256], BF16)
                        w2 = sc("w2", [128, 2, 128], BF16)
                        peT = sc("peT", [128, 32], BF16)
                        pesb = sc("pesb", [32, 128], BF16)
                        b1 = sc("b1", [128, 2], F32)
                        u = sc("u", [128, 448], F32)
                        u2 = sc("u2", [128, 448], F32)
                        hb_ = sc("hb", [128, 2, NCMP], BF16)
                        S.op("pool", lambda: nc.gpsimd.memset(hb_[:], 0.0), writes=[B("hb")])
                        for which in range(2):
                            S.dma("sp", tT[:], I["kvTw"][:, which, g, :, :].rearrange("r d t -> d r t"), reads=[], writes=[B("tT")], tag=B("tT"))
                            S.dma("pool", w1[:], I["cw1"][which].rearrange("(l p) n -> p l n", p=128), reads=[], writes=[B("w1")], tag=B("w1"))
                            S.dma("pool", w2[:], I["cw2"][which].rearrange("(c p) n -> p c n", p=128), reads=[], writes=[B("w2")], tag=B("w2"))
                            S.dma("pool", pesb[:], I["cpe"][which], reads=[], writes=[B("pesb")], tag=B("pesb"))
                            S.op("pe", lambda: nc.tensor.transpose(selTp[:, 0, 0:32], pesb[:, :], idb[0:32, 0:32]),
                                 reads=[B("pesb"), B("idb")], writes=[B("selTp")])
                            S.op("dve", lambda: nc.vector.tensor_copy(peT[:], selTp[:, 0, 0:32]), reads=[B("selTp")], writes=[B("peT")])
                            tflat = tT[:].rearrange("d r t -> d (r t)")
                            for hc in range(2):
                                xb_ = xbank[0]
                                for l in range(32):
                                    S.op("pe", lambda: nc.tensor.matmul(xb_[:, 0:1], w1[:, l, hc * 128:(hc + 1) * 128], peT[:, l:l + 1],
                                                                        start=(l == 0), stop=(l == 31)),
                                         reads=[B("w1"), B("peT")], writes=[B("ob0")])
                                S.op("dve", lambda: nc.vector.tensor_copy(b1[:, hc:hc + 1], xb_[:, 0:1]), reads=[B("ob0")], writes=[B("b1")])
                            for cb in range(2):
                                c0 = cb * 448
                                ncol = 448 if cb == 0 else 447
                                for hc in range(2):
                                    xb_ = xbank[1]
                                    for l in range(32):
                                        rhs = tflat.rearrange("d (c s) -> d c s", s=16)[:, c0 + l // 16:c0 + l // 16 + ncol, l % 16]
                                        S.op("pe", lambda: nc.tensor.matmul(xb_[:, 0:ncol], w1[:, l, hc * 128:(hc + 1) * 128], rhs,
                                                                            start=(l == 0), stop=(l == 31)),
                                             reads=[B("w1"), B("tT")], writes=[B("ob1")])
                                    S.op("dve", lambda: nc.vector.tensor_scalar(u[:, 0:ncol], xb_[:, 0:ncol], b1[:, hc:hc + 1], None, ALU.add),
                                         reads=[B("ob1"), B("b1")], writes=[B("u")])
                                    S.op("dve", lambda: nc.vector.tensor_tensor(u2[:, 0:ncol], u[:, 0:ncol], u[:, 0:ncol], ALU.mult),
                                         reads=[B("u")], writes=[B("u2")])
                                    S.op("dve", lambda: nc.vector.tensor_scalar(u2[:, 0:ncol], u2[:, 0:ncol], 0.044715, 1.0, ALU.mult, ALU.add),
                                         reads=[B("u2")], writes=[B("u2")])
                                    S.op("dve", lambda: nc.vector.tensor_tensor(u2[:, 0:ncol], u2[:, 0:ncol], u[:, 0:ncol], ALU.mult),
                                         reads=[B("u2"), B("u")], writes=[B("u2")])
                                    S.op("act", lambda: nc.scalar.activation(u2[:, 0:ncol], u2[:, 0:ncol], AF.Sigmoid, scale=1.5957691216057308),
                                         reads=[B("u2")], writes=[B("u2")])
                                    S.op("dve", lambda: nc.vector.tensor_tensor(hb_[:, hc, c0:c0 + ncol], u2[:, 0:ncol], u[:, 0:ncol], ALU.mult),
                                         reads=[B("u2"), B("u")], writes=[B("hb")])
                            if which == 0:
                                for cb in range(2):
                                    xb_ = xbank[0]
                                    for hc in range(2):
                                        S.op("pe", lambda: nc.tensor.matmul(xb_[:, 0:448], w2[:, hc, :], hb_[:, hc, cb * 448:(cb + 1) * 448],
                                                                            start=(hc == 0), stop=(hc == 1)),
                                             reads=[B("w2"), B("hb")], writes=[B("ob0")])
                                    S.op("dve", lambda: nc.vector.tensor_copy(kcmpT[:, cb * 448:(cb + 1) * 448], xb_[:, 0:448]),
                                         reads=[B("ob0")], writes=[B("kcmpT")])
                            else:
                                for bb in range(7):
                                    xb_ = xbank[bb % 2]
                                    xn = B(f"ob{bb % 2}")
                                    for hc in range(2):
                                        S.op("pe", lambda: nc.tensor.matmul(xb_[:, 0:128], hb_[:, hc, bb * 128:(bb + 1) * 128], w2[:, hc, :],
                                                                            start=(hc == 0), stop=(hc == 1)),
                                             reads=[B("w2"), B("hb")], writes=[xn])
                                    S.op("dve", lambda: nc.vector.tensor_scalar_mul(vcmp[:, bb, 0:128], xb_[:, 0:128], cval[:, bb:bb + 1]),
                                         reads=[xn, B("cval")], writes=[B("vcmp")])
                                    S.op("dve", lambda: nc.vector.tensor_copy(vcmp[:, bb, 128:129], cval[:, bb:bb + 1]),
                                         reads=[B("cval")], writes=[B("vcmp")])
                    S.drain()
                    if b_stage < 2:
                        continue
                    ksflat = ksT[:].rearrange("d r t -> d (r t)")
                    for i in b_qblocks:
                        qi = (g * 16 + i) % 2
                        qn = B(f"qB{qi}")
                        S.dma("sp", qB[qi][:], I["qbT"][g * 8:(g + 1) * 8, :, i * 128:(i + 1) * 128].rearrange("h d t -> d h t"),
                              reads=[], writes=[qn], tag=qn)
                        S.dma("sp", glt[qi][:], I["gl"][i * 128:(i + 1) * 128, :], reads=[], writes=[B(f"gl{qi}")], tag=B(f"gl{qi}"))
                        S.op("act", lambda: nc.scalar.activation(gs[:], glt[qi][:], AF.Sigmoid), reads=[B(f"gl{qi}")], writes=[B("gs")])
                        S.op("dve", lambda: nc.vector.tensor_copy(impa[:], fb[:, i, :]), reads=[B("fb")], writes=[B("impa")])
                        obi = nobst % 2
                        nobst += 1
                        ob_t = obst[obi]
                        obn = B(f"obst{obi}")

                        def finish_head(oacc, on, hl, br, first_branch, imp_lo=None):
                            h = g * 8 + hl
                            ri = cnt["rd"] % 2
                            cnt["rd"] += 1
                            rdn = B(f"rd{ri}")
                            S.op("dve", lambda: nc.vector.tensor_scalar(rd[ri][:, 2:3], oacc[:, 128:129], 1e-30, None, ALU.max),
                                 reads=[on], writes=[rdn])
                            S.op("dve", lambda: nc.vector.reciprocal(rd[ri][:, 0:1], rd[ri][:, 2:3]), reads=[rdn], writes=[rdn])
                            S.op("dve", lambda: nc.vector.scalar_tensor_tensor(rd[ri][:, 0:1], rd[ri][:, 2:3], 1e-20, rd[ri][:, 0:1], ALU.is_gt, ALU.mult),
                                 reads=[rdn], writes=[rdn])
                            S.op("dve", lambda: nc.vector.tensor_tensor(rd[ri][:, 1:2], rd[ri][:, 0:1], gs[:, 3 * h + br:3 * h + br + 1], ALU.mult),
                                 reads=[rdn, B("gs")], writes=[rdn])
                            if imp_lo is not None:
                                S.op("act", lambda: nc.scalar.activation(work[:, imp_lo:225], oacc[:, 132 + imp_lo:132 + 225], AF.Copy, scale=rd[ri][:, 0:1]),
                                     reads=[on, rdn], writes=[B("work")])
                                S.op("pool", lambda: nc.gpsimd.tensor_tensor(impa[:, imp_lo:225], impa[:, imp_lo:225], work[:, imp_lo:225], ALU.add),
                                     reads=[B("work"), B("impa")], writes=[B("impa")])
                            if first_branch:
                                S.op("act", lambda: nc.scalar.activation(ob_t[:, hl, :], oacc[:, 0:128], AF.Copy, scale=rd[ri][:, 1:2]),
                                     reads=[on, rdn], writes=[obn])
                            else:
                                oi_ = cnt["otmp"] % 2
                                cnt["otmp"] += 1
                                S.op("act", lambda: nc.scalar.activation(otmp[oi_][:], oacc[:, 0:128], AF.Copy, scale=rd[ri][:, 1:2]),
                                     reads=[on, rdn], writes=[B(f"otmp{oi_}")])
                                S.op("pool", lambda: nc.gpsimd.tensor_tensor(ob_t[:, hl, :], ob_t[:, hl, :], otmp[oi_][:], ALU.add),
                                     reads=[B(f"otmp{oi_}"), obn], writes=[obn])

                        for hl in range(8 if b_sub >= 1 else 0):
                            h = g * 8 + hl
                            sl = SLOPES[h]
                            oacc, on = next_ob()
                            nb = NBC[h]
                            for bo in range(nb):
                                bblk = 6 - bo
                                tix = (0 if bo == 0 else 1) * 16 + i

                                def tab(tmp, st, stn, tmpn, tix=tix, sl=sl):
                                    S.op("dve", lambda: nc.vector.scalar_tensor_tensor(tmp, cd[:, tix, :], -sl, st, ALU.mult, ALU.add),
                                         reads=[stn, B("cd")], writes=[tmpn])

                                unit(kcmpT[:, bblk * 128:(bblk + 1) * 128], qB[qi][:, hl, :], vcmp[:, bblk, :], oacc, on,
                                     bo == 0, bo == nb - 1, [B("kcmpT"), qn, B("vcmp")], tab,
                                     (0.0 if bo == 0 else -sl * 2048 * (bo - 1)),
                                     extra=((oacc[:, 132 + 32 * bblk:132 + 32 * bblk + 33], ov[:], [B("ov")]) if b_sub >= 2 else None))
                            pend.append(("f", functools.partial(finish_head, oacc, on, hl, 0, True, imp_lo=32 * (7 - nb))))
                        flush()
                        if b_stage < 3:
                            continue
                        S.op("dve", lambda: nc.vector.max(m8[:, 0:8], impa[:]), reads=[B("impa")], writes=[B("m8")])
                        S.op("dve", lambda: nc.vector.match_replace(work[:], m8[:, 0:8], impa[:], -1.0e30),
                             reads=[B("impa"), B("m8")], writes=[B("work")])
                        S.op("dve", lambda: nc.vector.max(m8[:, 8:16], work[:]), reads=[B("work")], writes=[B("m8")])
                        S.op("dve", lambda: nc.vector.tensor_scalar(sel[:], impa[:], m8[:, 15:16], None, ALU.is_ge),
                             reads=[B("impa"), B("m8")], writes=[B("sel")])
                        S.op("dve", lambda: nc.vector.scalar_tensor_tensor(sel[:], impa[:], -500.0, sel[:], ALU.is_gt, ALU.mult),
                             reads=[B("impa"), B("sel")], writes=[B("sel")])
                        S.op("dve", lambda: nc.vector.tensor_scalar(selb[:], sel[:], -1.0, None, ALU.add),
                             reads=[B("sel")], writes=[B("selb")])
                        for c2 in range(2):
                            S.op("pe", lambda: nc.tensor.transpose(selTp[:, c2, :], selb[:, c2 * 128:(c2 + 1) * 128], idb[:]),
                                 reads=[B("selb"), B("idb")], writes=[B("selTp")])
                        S.op("act", lambda: nc.scalar.copy(selT[:], selTp[:]), reads=[B("selTp")], writes=[B("selT")])
                        if b_stage < 4:
                            continue
                        for hl in range(8):
                            h = g * 8 + hl
                            sl = SLOPES[h]
                            oacc, on = next_ob()
                            nb = NBS[h]
                            for off in range(nb):
                                wbk = 96 + i - off
                                ty = 1 if off == 0 else 0

                                def tab(tmp, st, stn, tmpn, ty=ty, sl=sl):
                                    S.op("dve", lambda: nc.vector.scalar_tensor_tensor(tmp, dqk[:, ty, :], -sl, st, ALU.mult, ALU.add),
                                         reads=[stn, B("dqk")], writes=[tmpn])

                                ch = wbk // 64
                                unit(ksflat[:, wbk * 128:(wbk + 1) * 128], qB[qi][:, hl, :], vs[:, wbk, :], oacc, on,
                                     off == 0, off == nb - 1, [B("ksT"), qn, B("vs")], tab, -sl * 128 * off,
                                     mask=(wt[:, 128 * (wbk % 64):128 * (wbk % 64) + 128], selT[:, ch, :], [B("wt"), B("selT")]))
                            pend.append(("f", functools.partial(finish_head, oacc, on, hl, 1, False)))
                        if b_stage < 5:
                            continue
                        for hl in range(8):
                            h = g * 8 + hl
                            sl = SLOPES[h]
                            oacc, on = next_ob()
                            for off in range(5):
                                wbk = 4 + i - off
                                ty = 1 if off == 0 else 2 if off == 4 else 0

                                def tab(tmp, st, stn, tmpn, ty=ty, sl=sl):
                                    S.op("dve", lambda: nc.vector.scalar_tensor_tensor(tmp, dqk[:, ty, :], -sl, st, ALU.mult, ALU.add),
                                         reads=[stn, B("dqk")], writes=[tmpn])

                                unit(kwT[:, wbk * 128:(wbk + 1) * 128], qB[qi][:, hl, :], vw[:, wbk, :], oacc, on,
                                     off == 0, off == 4, [B("kwT"), qn, B("vw")], tab, -sl * 128 * off)
                            pend.append(("f", functools.partial(finish_head, oacc, on, hl, 2, False)))
                        flush()
                        S.dma("sp", O["ob"][i * 128:(i + 1) * 128, g * 1024:(g + 1) * 1024], ob_t[:].rearrange("q h d -> q (h d)"),
                              reads=[obn], writes=[], tag=obn)


def emit_p3(nc, S, x, oa, ob, z, goa, gob, wout, ident, xnew, pfx="p3"):
    B = lambda n: pfx + n
    TP = 1024
    NT = TP // 128
    KC = D // 128
    with contextlib.ExitStack() as es:
        sb = lambda name, shape, dt: es.enter_context(nc.sbuf_tensor(pfx + name, shape, dt))
        yT = sb("yT", [128, KC, TP], BF16)
        wb = [sb(f"wb{i}", [128, KC, 512], BF16) for i in range(2)]
        gO = sb("gO", [128, D], F32)
        ot = [sb(f"ot{i}", [128, 2048], F32) for i in range(2)]
        zt = [sb(f"zt{i}", [128, 2048], F32) for i in range(2)]
        yb = sb("yb", [128, 2048], BF16)
        idb = sb("idb", [128, 128], BF16)
        ss = sb("ss", [128, 2], F32)
        xr = [sb(f"xr{i}", [128, 512], F32) for i in range(3)]
        xo = [sb(f"xo{i}", [128, 512], F32) for i in range(3)]
        acc = [es.enter_context(nc.psum_tensor(pfx + f"acc{i}", [128, 512], F32)) for i in range(6)]
        tps = [es.enter_context(nc.psum_tensor(pfx + f"tp{i}", [128, 512], BF16)) for i in range(2)]
        S.dma("sp", idb[:], ident[:, :], reads=[], writes=[B("idb")], tag=B("idb"))
        S.dma("sp", gO[:, 0:2048], goa.partition_broadcast(128), reads=[], writes=[B("gO")], tag=B("gO"))
        S.dma("sp", gO[:, 2048:4096], gob.partition_broadcast(128), reads=[], writes=[B("gO")], tag=B("gO"))
        cnt = {"acc": 0, "tp": 0, "w": 0, "x": 0, "ld": 0}
        for p in range(TOK // TP):
            t0 = p * TP
            for tt in range(NT):
                tok0 = t0 + tt * 128
                for half, osrc in ((0, oa), (1, ob)):
                    li = cnt["ld"] % 2
                    cnt["ld"] += 1
                    o_t, z_t = ot[li], zt[li]
                    on, zn = B(f"ot{li}"), B(f"zt{li}")
                    S.dma("sp", o_t[:], osrc[tok0:tok0 + 128, :], reads=[], writes=[on], tag=on)
                    S.dma("sp", z_t[:], z[tok0:tok0 + 128, half * 2048:(half + 1) * 2048], reads=[], writes=[zn], tag=zn)
                    S.op("act", lambda: nc.scalar.activation(yb[:], o_t[:], AF.Square, accum_out=ss[:, 0:1]),
                         reads=[on], writes=[B("yb"), B("ss")])
                    S.op("dve", lambda: nc.vector.tensor_scalar(ss[:, 1:2], ss[:, 0:1], 1.0 / 2048, EPS, ALU.mult, ALU.add),
                         reads=[B("ss")], writes=[B("ss")])
                    S.op("act", lambda: nc.scalar.activation(ss[:, 1:2], ss[:, 1:2], AF.Sqrt), reads=[B("ss")], writes=[B("ss")])
                    S.op("dve", lambda: nc.vector.reciprocal(ss[:, 1:2], ss[:, 1:2]), reads=[B("ss")], writes=[B("ss")])
                    S.op("act", lambda: nc.scalar.activation(z_t[:], z_t[:], AF.Silu), reads=[zn], writes=[zn])
                    S.op("pool", lambda: nc.gpsimd.tensor_tensor(z_t[:], z_t[:], gO[:, half * 2048:(half + 1) * 2048], ALU.mult),
                         reads=[zn, B("gO")], writes=[zn])
                    S.op("dve", lambda: nc.vector.tensor_scalar_mul(o_t[:], o_t[:], ss[:, 1:2]), reads=[on, B("ss")], writes=[on])
                    S.op("dve", lambda: nc.vector.tensor_tensor(yb[:], o_t[:], z_t[:], ALU.mult), reads=[on, zn], writes=[B("yb")])
                    for k4 in range(4):
                        tpi = cnt["tp"] % 2
                        cnt["tp"] += 1
                        tp = tps[tpi]
                        for j in range(4):
                            kc = k4 * 4 + j
                            S.op("pe", lambda: nc.tensor.transpose(tp[:, j * 128:(j + 1) * 128], yb[:, kc * 128:(kc + 1) * 128], idb[:]),
                                 reads=[B("yb"), B("idb")], writes=[B(f"tp{tpi}")])
                        dst = yT[:, half * 16 + k4 * 4:half * 16 + k4 * 4 + 4, tt * 128:(tt + 1) * 128]
                        src = tp[:].rearrange("p (j t) -> p j t", t=128)
                        if k4 % 2 == 0:
                            S.op("dve", lambda: nc.vector.tensor_copy(dst, src), reads=[B(f"tp{tpi}")], writes=[B("yT")])
                        else:
                            S.op("act", lambda: nc.scalar.copy(dst, src), reads=[B(f"tp{tpi}")], writes=[B("yT")])
            for ch in range(8):
                wi = cnt["w"] % 2
                cnt["w"] += 1
                w, wn = wb[wi], B(f"wb{wi}")
                S.dma("pool", w[:], wout[:, ch * 512:(ch + 1) * 512].rearrange("(kc p) n -> p kc n", p=128),
                      reads=[], writes=[wn], tag=wn)
                for tt in range(NT):
                    tok0 = t0 + tt * 128
                    ai = cnt["acc"] % 6
                    cnt["acc"] += 1
                    a, an = acc[ai], B(f"acc{ai}")
                    xi = cnt["x"] % 3
                    cnt["x"] += 1
                    xn = B(f"xr{xi}")
                    S.dma("sp", xr[xi][:], x[tok0:tok0 + 128, ch * 512:(ch + 1) * 512], reads=[], writes=[xn], tag=xn)
                    for kc in range(KC):
                        S.op("pe", lambda: nc.tensor.matmul(a[:], yT[:, kc, tt * 128:(tt + 1) * 128], w[:, kc, :],
                                                            start=(kc == 0), stop=(kc == KC - 1)),
                             reads=[wn, B("yT")], writes=[an])
                    xon = B(f"xo{xi}")
                    S.op("dve", lambda: nc.vector.tensor_tensor(xo[xi][:], a[:], xr[xi][:], ALU.add), reads=[an, xn], writes=[xon])
                    S.dma("sp", xnew[tok0:tok0 + 128, ch * 512:(ch + 1) * 512], xo[xi][:], reads=[xon], writes=[], tag=xon)


def emit_p4(nc, S, xin, fg, out, pfx="p4"):
    B = lambda n: pfx + n
    with contextlib.ExitStack() as es:
        sb = lambda name, shape, dt: es.enter_context(nc.sbuf_tensor(pfx + name, shape, dt))
        gF = sb("gF", [128, D], F32)
        xt = [sb(f"xt{i}", [128, D], F32) for i in range(2)]
        junk = sb("junk", [128, D], BF16)
        ss = sb("ss", [128, 2], F32)
        S.dma("sp", gF[:], fg.partition_broadcast(128), reads=[], writes=[B("gF")], tag=B("gF"))
        for tt in range(TOK // 128):
            xi = tt % 2
            xn = B(f"xt{xi}")
            S.dma("sp", xt[xi][:], xin[tt * 128:(tt + 1) * 128, :], reads=[], writes=[xn], tag=xn)
            S.op("act", lambda: nc.scalar.activation(junk[:], xt[xi][:], AF.Square, accum_out=ss[:, 0:1]),
                 reads=[xn], writes=[B("junk"), B("ss")])
            S.op("dve", lambda: nc.vector.tensor_scalar(ss[:, 1:2], ss[:, 0:1], 1.0 / D, EPS, ALU.mult, ALU.add),
                 reads=[B("ss")], writes=[B("ss")])
            S.op("act", lambda: nc.scalar.activation(ss[:, 1:2], ss[:, 1:2], AF.Sqrt), reads=[B("ss")], writes=[B("ss")])
            S.op("dve", lambda: nc.vector.reciprocal(ss[:, 1:2], ss[:, 1:2]), reads=[B("ss")], writes=[B("ss")])
            S.op("dve", lambda: nc.vector.tensor_scalar_mul(xt[xi][:], xt[xi][:], ss[:, 1:2]), reads=[xn, B("ss")], writes=[xn])
            S.op("dve", lambda: nc.vector.tensor_tensor(xt[xi][:], xt[xi][:], gF[:], ALU.mult), reads=[xn, B("gF")], writes=[xn])
            S.dma("sp", out[tt * 128:(tt + 1) * 128, :], xt[xi][:], reads=[xn], writes=[], tag=xn)


P2_IN_SPECS = {
    "qaT": ([NHA, 128, TOK], BF16), "qbT": ([NHB, 128, TOK], BF16), "gl": ([TOK, 48], F32),
    "kaTw": ([2, NHA, 128, TOK], BF16), "vaw": ([2, TOK, NHA, 129], BF16),
    "kvTw": ([7, 4, 2, 128, TOK], BF16), "vsww": ([7, 2, TOK, 2, 129], BF16),
    "cw1": ([2, 4096, 256], F32), "cw2": ([2, 256, 128], F32), "cpe": ([2, 32, 128], F32),
}


def window(arr, core, nback):
    out = np.zeros((nback + 1,) + arr.shape[1:], arr.dtype)
    for j in range(nback + 1):
        r = core - nback + j
        if r >= 0:
            out[j] = arr[r]
    return out


def p2_inputs(core, own, G, cw, tables):
    m = {"qaT": own["qaT"], "qbT": own["qbT"], "gl": own["gl"],
         "kaTw": window(G["kaT"], core, 1), "vaw": window(G["va"], core, 1),
         "kvTw": window(G["kvT"], core, NPADR), "vsww": window(G["vsw"], core, NPADR)}
    m.update(cw)
    m.update(tables)
    return m


def build_p2_only(do_a=True, do_b=True, **kw):
    nc = bass.Bass("TRN2", target_bir_lowering=False)
    I = {k: nc.dram_tensor(k, shp, dt, kind="ExternalInput").ap() for k, (shp, dt) in {**P2_IN_SPECS, **TABLE_SPECS}.items()}
    O = {k: nc.dram_tensor(k, [TOK, 2048], F32, kind="ExternalOutput").ap() for k in ("oa", "ob")}
    S = Sched(nc)
    emit_p2(nc, S, I, O, do_a=do_a, do_b=do_b, **kw)
    S.drain()
    return nc


def _dram_in(nc, specs):
    return {k: nc.dram_tensor(k, shp, dt, kind="ExternalInput").ap() for k, (shp, dt) in specs.items()}


POST_IN_SPECS = {"x": ([TOK, D], F32), "z": ([TOK, D], F32), "goa": ([2048], F32), "gob": ([2048], F32),
                 "wout": ([D, D], F32)}


def build_first():
    return build_p1_only()


def build_mid():
    nc = bass.Bass("TRN2", target_bir_lowering=False)
    I = _dram_in(nc, {**P2_IN_SPECS, **TABLE_SPECS, **POST_IN_SPECS, "ng": ([D], F32), "win": ([D, DIN], F32)})
    o = {k: nc.dram_tensor("n_" + k, shp, dt, kind="ExternalOutput").ap() for k, (shp, dt) in p1_out_specs().items()}
    xnew = nc.dram_tensor("xnew", [TOK, D], F32, kind="ExternalOutput").ap()
    O = {k: nc.dram_tensor("s_" + k, [TOK, 2048], F32).ap() for k in ("oa", "ob")}
    S = Sched(nc)
    emit_p2(nc, S, I, O)
    S.drain()
    emit_p3(nc, S, I["x"], O["oa"], O["ob"], I["z"], I["goa"], I["gob"], I["wout"], I["ident"], xnew)
    S.drain()
    emit_p1(nc, S, xnew, I["ng"], I["win"], I["ident"], o)
    S.drain()
    return nc


def build_last():
    nc = bass.Bass("TRN2", target_bir_lowering=False)
    I = _dram_in(nc, {**P2_IN_SPECS, **TABLE_SPECS, **POST_IN_SPECS, "fg": ([D], F32)})
    out = nc.dram_tensor("out", [TOK, D], F32, kind="ExternalOutput").ap()
    xnew = nc.dram_tensor("s_xnew", [TOK, D], F32).ap()
    O = {k: nc.dram_tensor("s_" + k, [TOK, 2048], F32).ap() for k in ("oa", "ob")}
    S = Sched(nc)
    emit_p2(nc, S, I, O)
    S.drain()
    emit_p3(nc, S, I["x"], O["oa"], O["ob"], I["z"], I["goa"], I["gob"], I["wout"], I["ident"], xnew)
    S.drain()
    emit_p4(nc, S, xnew, I["fg"], out)
    S.drain()
    return nc


def build_p3_test():
    nc = bass.Bass("TRN2", target_bir_lowering=False)
    I = _dram_in(nc, {**POST_IN_SPECS, "oa": ([TOK, 2048], F32), "ob": ([TOK, 2048], F32), "fg": ([D], F32),
                      "ident": ([128, 128], BF16)})
    out = nc.dram_tensor("out", [TOK, D], F32, kind="ExternalOutput").ap()
    xnew = nc.dram_tensor("xnew", [TOK, D], F32, kind="ExternalOutput").ap()
    S = Sched(nc)
    emit_p3(nc, S, I["x"], I["oa"], I["ob"], I["z"], I["goa"], I["gob"], I["wout"], I["ident"], xnew)
    S.drain()
    emit_p4(nc, S, xnew, I["fg"], out)
    S.drain()
    return nc


def _run(nc, in_maps):
    res = run_bass_kernel_spmd(nc, in_maps, core_ids=list(range(NCORES)))
    return res.results


def kernel(x, norm_g, w_in, cmp_k_pe, cmp_k_w1, cmp_k_w2, cmp_v_pe, cmp_v_w1, cmp_v_w2,
           out_g_a, out_g_b, w_out, final_g):
    f32 = lambda a: np.ascontiguousarray(np.asarray(a, dtype=np.float32))
    x2 = f32(x).reshape(SEQ, D)
    norm_g, w_in, w_out, final_g = f32(norm_g), f32(w_in), f32(w_out), f32(final_g)
    out_g_a, out_g_b = f32(out_g_a), f32(out_g_b)
    cw1 = np.stack([f32(cmp_k_w1), f32(cmp_v_w1)], 1)
    cw2 = np.stack([f32(cmp_k_w2), f32(cmp_v_w2)], 1)
    cpe = np.stack([f32(cmp_k_pe), f32(cmp_v_pe)], 1)
    tables = [make_tables(c) for c in range(NCORES)]
    ident = tables[0]["ident"]
    xcur = [x2[c * TOK:(c + 1) * TOK] for c in range(NCORES)]
    nc = build_first()
    res = _run(nc, [{"x": xcur[c], "ng": norm_g[0], "win": w_in[0], "ident": ident} for c in range(NCORES)])
    nc_mid = None
    out = None
    for l in range(DEPTH):
        G = {k: np.stack([np.asarray(res[c][k]) for c in range(NCORES)]) for k in ("kaT", "va", "kvT", "vsw")}
        cw = {"cw1": cw1[l], "cw2": cw2[l], "cpe": cpe[l]}
        in_maps = []
        for c in range(NCORES):
            m = p2_inputs(c, res[c], G, cw, tables[c])
            m.update(x=xcur[c], z=np.asarray(res[c]["z"]), goa=out_g_a[l], gob=out_g_b[l], wout=w_out[l])
            if l < DEPTH - 1:
                m.update(ng=norm_g[l + 1], win=w_in[l + 1])
            else:
                m.update(fg=final_g)
            in_maps.append(m)
        if l < DEPTH - 1:
            if nc_mid is None:
                nc_mid = build_mid()
            r = _run(nc_mid, in_maps)
            xcur = [np.asarray(r[c]["xnew"]) for c in range(NCORES)]
            res = [{k: np.asarray(r[c]["n_" + k]) for k in p1_out_specs()} for c in range(NCORES)]
        else:
            r = _run(build_last(), in_maps)
            out = np.concatenate([np.asarray(r[c]["out"]) for c in range(NCORES)], 0)
    return out.reshape(1, SEQ, D).astype(np.float32)
```

```python
import contextlib
import functools
import numpy as np
import ml_dtypes
import concourse.bass as bass
import concourse.mybir as mybir
from concourse.bass_utils import run_bass_kernel_spmd

F32 = mybir.dt.float32
BF16 = mybir.dt.bfloat16
AF = mybir.ActivationFunctionType
ALU = mybir.AluOpType
AX = mybir.AxisListType

NCORES = 8
D = 4096
SEQ = 16384
TOK = SEQ // NCORES
HD = 128
NHA = 16
NHB = 16
DIN = 13872
DEPTH = 4
EPS = 1e-6
QSCALE = HD ** -0.5
C_QA, C_KA, C_VA, C_ZA, C_QB = 0, 2048, 4096, 6144, 8192
C_KC, C_VC, C_KS, C_VS, C_KW, C_VW, C_ZB, C_G = 10240, 10496, 10752, 11008, 11264, 11520, 11776, 13824


class Sched:
    def __init__(self, nc):
        self.nc = nc
        self.eng = {"pe": nc.tensor, "dve": nc.vector, "act": nc.scalar, "pool": nc.gpsimd, "sp": nc.sync}
        self.esem, self.ecnt = {}, {}
        self.seen = {k: {} for k in self.eng}
        self.bufs = {}
        self.dsem, self.dcnt = {}, {}
        self._ctx = contextlib.ExitStack()
        for k in self.eng:
            self.esem[k] = self._ctx.enter_context(nc.semaphore("s_" + k))
            self.ecnt[k] = 0

    def _buf(self, b):
        st = self.bufs.get(b)
        if st is None:
            st = self.bufs[b] = [{}, {}]
        return st

    def _deps(self, reads, writes):
        d = {}

        def add(m):
            for k, sv in m.items():
                if k not in d or d[k][1] < sv[1]:
                    d[k] = sv

        for b in reads:
            add(self._buf(b)[0])
        for b in writes:
            w, r = self._buf(b)
            add(w)
            add(r)
        return d

    def _wait(self, e, deps, skip_self=False):
        seen = self.seen[e]
        for k, (s, v) in deps.items():
            if skip_self and k == "E" + e:
                continue
            if seen.get(k, -1) >= v:
                continue
            self.eng[e].wait_ge(s, v)
            seen[k] = v

    def _commit(self, reads, writes, key, sem, val):
        for b in reads:
            r = self._buf(b)[1]
            if key not in r or r[key][1] < val:
                r[key] = (sem, val)
        for b in writes:
            st = self._buf(b)
            st[0] = {key: (sem, val)}
            st[1] = {}

    def op(self, e, fn, reads=(), writes=()):
        deps = self._deps(reads, writes)
        self._wait(e, deps, skip_self=(e == "pe"))
        ins = fn()
        self.ecnt[e] += 1
        ins.then_inc(self.esem[e], 1)
        self._commit(reads, writes, "E" + e, self.esem[e], self.ecnt[e])
        return ins

    def dma(self, q, out, in_, reads, writes, tag, **kw):
        if tag not in self.dsem:
            self.dsem[tag] = self._ctx.enter_context(self.nc.semaphore("d_" + str(tag)))
            self.dcnt[tag] = 0
        deps = self._deps(reads, writes)
        self._wait(q, deps)
        ins = self.eng[q].dma_start(out=out, in_=in_, **kw)
        self.dcnt[tag] += 16
        ins.then_inc(self.dsem[tag], 16)
        self._commit(reads, writes, "D" + str(tag), self.dsem[tag], self.dcnt[tag])
        return ins

    def finish(self, e, bufs):
        self._wait(e, self._deps(bufs, bufs))

    def drain(self, engines=None):
        deps = {}
        for k in self.eng:
            if self.ecnt[k] > 0:
                deps["E" + k] = (self.esem[k], self.ecnt[k])
        for t, s_ in self.dsem.items():
            if self.dcnt[t] > 0:
                deps["D" + str(t)] = (s_, self.dcnt[t])
        for e in (engines or self.eng):
            self._wait(e, deps)


def emit_p1(nc, S, x, ng, win, ident, o, pfx="p1"):
    TP = 1024
    NT = TP // 128
    KC = D // 128
    with contextlib.ExitStack() as es:
        sb = lambda name, shape, dt: es.enter_context(nc.sbuf_tensor(pfx + name, shape, dt))
        hT = sb("hT", [128, KC, TP], BF16)
        wb = [sb(f"wb{i}", [128, KC, 512], BF16) for i in range(2)]
        xt = [sb(f"xt{i}", [128, D], F32) for i in range(2)]
        xs = sb("xs", [128, D], BF16)
        gT = sb("gT", [128, KC], F32)
        idb = sb("idb", [128, 128], BF16)
        ss = sb("ss", [128, 2], F32)
        stF = [sb(f"stF{i}", [128, 512], BF16) for i in range(3)]
        stV = [sb(f"stV{i}", [128, 4, 129], BF16) for i in range(3)]
        stZ = [sb(f"stZ{i}", [128, 512], F32) for i in range(3)]
        acc = [es.enter_context(nc.psum_tensor(pfx + f"acc{i}", [128, 512], F32)) for i in range(6)]
        tps = [es.enter_context(nc.psum_tensor(pfx + f"tp{i}", [128, 512], BF16)) for i in range(2)]
        B = lambda n: pfx + n

        nc_ = nc
        S.dma("sp", idb[:], ident[:, :], reads=[], writes=[B("idb")], tag=B("idb"))
        with nc.allow_non_contiguous_dma(reason="tiny gain vector"):
            S.dma("sp", gT[:], ng.rearrange("(c p) -> p c", p=128), reads=[], writes=[B("gT")], tag=B("gT"))
        for i in range(3):
            S.op("dve", lambda: nc.vector.memset(stV[i][:], 1.0), writes=[B(f"stV{i}")])

        cnt = {"acc": 0, "tp": 0, "F": 0, "V": 0, "Z": 0, "w": 0, "ev": 0}

        def evac(dst, src, reads, writes, scale=None):
            cnt["ev"] += 1
            if cnt["ev"] % 2 == 0:
                if scale is None:
                    S.op("dve", lambda: nc.vector.tensor_copy(dst, src), reads=reads, writes=writes)
                else:
                    S.op("dve", lambda: nc.vector.tensor_scalar_mul(dst, src, scale), reads=reads, writes=writes)
            else:
                S.op("act", lambda: nc.scalar.activation(dst, src, AF.Copy, scale=(1.0 if scale is None else scale)),
                     reads=reads, writes=writes)

        for p in range(TOK // TP):
            t0 = p * TP
            for tt in range(NT):
                xb = xt[tt % 2]
                xbn = B(f"xt{tt%2}")
                S.dma("sp", xb[:], x[t0 + tt * 128:t0 + (tt + 1) * 128, :], reads=[], writes=[xbn], tag=xbn)
                S.op("act", lambda: nc.scalar.activation(xs[:], xb[:], AF.Square, accum_out=ss[:, 0:1]),
                     reads=[xbn], writes=[B("xs"), B("ss")])
                S.op("dve", lambda: nc.vector.tensor_scalar(ss[:, 1:2], ss[:, 0:1], 1.0 / D, EPS, ALU.mult, ALU.add),
                     reads=[B("ss")], writes=[B("ss")])
                S.op("act", lambda: nc.scalar.activation(ss[:, 1:2], ss[:, 1:2], AF.Sqrt),
                     reads=[B("ss")], writes=[B("ss")])
                S.op("dve", lambda: nc.vector.reciprocal(ss[:, 1:2], ss[:, 1:2]),
                     reads=[B("ss")], writes=[B("ss")])
                S.op("act", lambda: nc.scalar.activation(xs[:], xb[:], AF.Copy, scale=ss[:, 1:2]),
                     reads=[xbn, B("ss")], writes=[B("xs")])
                for k4 in range(KC // 4):
                    tpi = cnt["tp"] % 2
                    cnt["tp"] += 1
                    tp = tps[tpi]
                    for j in range(4):
                        kc = k4 * 4 + j
                        S.op("pe", lambda: nc.tensor.transpose(tp[:, j * 128:(j + 1) * 128], xs[:, kc * 128:(kc + 1) * 128], idb[:]),
                             reads=[B("xs"), B("idb")], writes=[B(f"tp{tpi}")])
                    for j in range(4):
                        kc = k4 * 4 + j
                        dst = hT[:, kc, tt * 128:(tt + 1) * 128]
                        src = tp[:, j * 128:(j + 1) * 128]
                        if j % 2 == 0:
                            S.op("dve", lambda: nc.vector.tensor_scalar_mul(dst, src, gT[:, kc:kc + 1]),
                                 reads=[B(f"tp{tpi}"), B("gT")], writes=[B("hT")])
                        else:
                            S.op("act", lambda: nc.scalar.activation(dst, src, AF.Copy, scale=gT[:, kc:kc + 1]),
                                 reads=[B(f"tp{tpi}"), B("gT")], writes=[B("hT")])

            def load_w(c0, ncols):
                wi = cnt["w"] % 2
                cnt["w"] += 1
                S.dma("pool", wb[wi][:, :, 0:ncols], win[:, c0:c0 + ncols].rearrange("(kc p) n -> p kc n", p=128),
                      reads=[], writes=[B(f"wb{wi}")], tag=B(f"wb{wi}"))
                return wb[wi], B(f"wb{wi}")

            def next_acc():
                ai = cnt["acc"] % 6
                cnt["acc"] += 1
                return acc[ai], B(f"acc{ai}")

            def feat_tile(w, wn, wc0, dst_fn, scale):
                for half in range(TP // 512):
                    a, an = next_acc()
                    for kc in range(KC):
                        S.op("pe", lambda: nc.tensor.matmul(a[:], w[:, kc, wc0:wc0 + 128], hT[:, kc, half * 512:(half + 1) * 512],
                                                            start=(kc == 0), stop=(kc == KC - 1)),
                             reads=[wn, B("hT")], writes=[an])
                    si = cnt["F"] % 3
                    cnt["F"] += 1
                    evac(stF[si][:], a[:], [an], [B(f"stF{si}")], scale)
                    S.dma("sp", dst_fn(t0 + half * 512, 512), stF[si][:], reads=[B(f"stF{si}")], writes=[], tag=B(f"stF{si}"))

            def tok_tiles(w, wn, wc0, ncols, kind, dst_fn):
                for tt in range(NT):
                    a, an = next_acc()
                    for kc in range(KC):
                        S.op("pe", lambda: nc.tensor.matmul(a[:, 0:ncols], hT[:, kc, tt * 128:(tt + 1) * 128], w[:, kc, wc0:wc0 + ncols],
                                                            start=(kc == 0), stop=(kc == KC - 1)),
                             reads=[wn, B("hT")], writes=[an])
                    tok0 = t0 + tt * 128
                    if kind == "V":
                        si = cnt["V"] % 3
                        cnt["V"] += 1
                        nh = ncols // 128
                        evac(stV[si][:, 0:nh, 0:128], a[:, 0:ncols].rearrange("p (h d) -> p h d", d=128),
                             [an], [B(f"stV{si}")])
                        S.dma("sp", dst_fn(tok0), stV[si][:, 0:nh, :], reads=[B(f"stV{si}")], writes=[], tag=B(f"stV{si}"))
                    else:
                        si = cnt["Z"] % 3
                        cnt["Z"] += 1
                        evac(stZ[si][:, 0:ncols], a[:, 0:ncols], [an], [B(f"stZ{si}")])
                        S.dma("sp", dst_fn(tok0), stZ[si][:, 0:ncols], reads=[B(f"stZ{si}")], writes=[], tag=B(f"stZ{si}"))

            for (c0, dst, scale) in ((C_QA, o["qaT"], QSCALE), (C_KA, o["kaT"], None), (C_QB, o["qbT"], QSCALE)):
                for ch in range(4):
                    w, wn = load_w(c0 + ch * 512, 512)
                    for sub in range(4):
                        h = ch * 4 + sub
                        feat_tile(w, wn, sub * 128, lambda tk, n, h=h, dst=dst: dst[h, :, tk:tk + n], scale)
            w, wn = load_w(C_KC, 512)
            for sub in range(4):
                feat_tile(w, wn, sub * 128, lambda tk, n, sub=sub: o["kvT"][sub // 2, sub % 2, :, tk:tk + n], None)
            w, wn = load_w(C_KS, 512)
            for sub in range(2):
                feat_tile(w, wn, sub * 128, lambda tk, n, sub=sub: o["kvT"][2, sub, :, tk:tk + n], None)
            tok_tiles(w, wn, 256, 256, "V", lambda tok0: o["vsw"][0, tok0:tok0 + 128, :, :])
            w, wn = load_w(C_KW, 512)
            for sub in range(2):
                feat_tile(w, wn, sub * 128, lambda tk, n, sub=sub: o["kvT"][3, sub, :, tk:tk + n], None)
            tok_tiles(w, wn, 256, 256, "V", lambda tok0: o["vsw"][1, tok0:tok0 + 128, :, :])
            for ch in range(4):
                w, wn = load_w(C_VA + ch * 512, 512)
                tok_tiles(w, wn, 0, 512, "V", lambda tok0, ch=ch: o["va"][tok0:tok0 + 128, ch * 4:(ch + 1) * 4, :])
            for (c0, zoff) in ((C_ZA, 0), (C_ZB, 2048)):
                for ch in range(4):
                    w, wn = load_w(c0 + ch * 512, 512)
                    tok_tiles(w, wn, 0, 512, "Z",
                              lambda tok0, ch=ch, zoff=zoff: o["z"][tok0:tok0 + 128, zoff + ch * 512:zoff + (ch + 1) * 512])
            w, wn = load_w(C_G, 48)
            tok_tiles(w, wn, 0, 48, "Z", lambda tok0: o["gl"][tok0:tok0 + 128, :])

        outbufs = [B(f"stF{i}") for i in range(3)] + [B(f"stV{i}") for i in range(3)] + [B(f"stZ{i}") for i in range(3)]
        return outbufs


def p1_out_specs():
    return {
        "qaT": ([NHA, 128, TOK], BF16), "kaT": ([NHA, 128, TOK], BF16), "va": ([TOK, NHA, 129], BF16),
        "qbT": ([NHB, 128, TOK], BF16), "kvT": ([4, 2, 128, TOK], BF16), "vsw": ([2, TOK, 2, 129], BF16),
        "z": ([TOK, D], F32), "gl": ([TOK, 48], F32),
    }


def build_p1_only():
    nc = bass.Bass("TRN2", target_bir_lowering=False)
    x = nc.dram_tensor("x", [TOK, D], F32, kind="ExternalInput").ap()
    ng = nc.dram_tensor("ng", [D], F32, kind="ExternalInput").ap()
    win = nc.dram_tensor("win", [D, DIN], F32, kind="ExternalInput").ap()
    ident = nc.dram_tensor("ident", [128, 128], BF16, kind="ExternalInput").ap()
    o = {k: nc.dram_tensor(k, shp, dt, kind="ExternalOutput").ap() for k, (shp, dt) in p1_out_specs().items()}
    S = Sched(nc)
    bufs = emit_p1(nc, S, x, ng, win, ident, o)
    S.finish("sp", bufs)
    return nc


SLOPES = [2.0 ** (-(k + 1) / 2.0) for k in range(16)]
NEGBIG = -30000.0
POSBIG = 1.0e9
NEED = [32.0 / s for s in SLOPES]
NBS = [min(int(np.ceil(n / 128.0)) + 2, 97) for n in NEED]
NBC = [min(int((n + 16 + 2063) // 2048) + 1, 7) for n in NEED]
NPADR = 6
WTOK = (NPADR + 1) * TOK
NCMP = WTOK // 16
NJW = WTOK // 64


def a_type(off):
    return 0 if off == 0 else 1 if off == 1 else 2 if off in (2, 3) else 3 if off == 4 else 4 if off < 16 else 5


def make_tables(core):
    k = np.arange(128)[:, None].astype(np.float64)
    q = np.arange(128)[None, :].astype(np.float64)
    dqk = q - k
    t = {}
    dq = np.zeros((3, 128, 128), np.float32)
    dq[0] = dqk
    dq[1] = np.where(dqk >= 0, dqk, POSBIG)
    dq[2] = np.where(dqk < 0, dqk, POSBIG)
    t["dqk"] = dq
    t["dqk4"] = np.stack([dqk + 128 * b for b in range(4)], 0).astype(np.float32)
    lm = np.zeros((6, 128, 128), np.float32)
    for ty, off in enumerate((0, 1, 2, 4, 5, 16)):
        dl = 128 * off + dqk
        mult = ((dl >= 0) & (dl <= 128)).astype(np.float64)
        mult += ((dl >= 0) & (dl % 4 == 0) & (dl <= 512))
        mult += ((dl >= 0) & (dl % 16 == 0) & (dl <= 2048))
        if ty == 2:
            dl3 = 128 * 3 + dqk
            m3 = ((dl3 % 4 == 0) & (dl3 <= 512)).astype(np.float64) + ((dl3 % 16 == 0) & (dl3 <= 2048))
            assert np.array_equal(m3, mult)
        lm[ty] = np.where(mult > 0, np.log(np.maximum(mult, 1)), NEGBIG)
    t["lm"] = lm
    t["lm4"] = np.ascontiguousarray(np.stack([lm[4]] * 4, 0))
    cd = np.zeros((2, 16, 128, 128), np.float32)
    for i in range(16):
        d0 = q - 16 * k + 128 * i - 31
        cd[0, i] = np.where(d0 >= 0, d0, POSBIG)
        d1 = d0 + 2048
        cd[1, i] = np.where(d1 >= 0, d1, POSBIG)
    t["cd"] = cd
    ov = np.zeros((128, 33), np.float32)
    for m in range(33):
        ov[:, m] = ((k[:, 0] >= 4 * m - 1) & (k[:, 0] <= 4 * m + 3))
    t["ov"] = ov.astype(ml_dtypes.bfloat16)
    wt = np.zeros((128, 8192), np.float32)
    u = np.arange(8192)
    wt[u // 64, u] = -NEGBIG
    t["wt"] = wt.astype(ml_dtypes.bfloat16)
    fb = np.zeros((16, 128, 256), np.float32)
    jw = np.arange(256)[None, :]
    for i in range(16):
        curw = 192 + 2 * i + (np.arange(128)[:, None] >= 64)
        jabs = jw - 192 + 32 * core
        valid = (jabs >= 0) & (jw <= curw) & (jw < NJW)
        forced = valid & ((jabs == 0) | ((curw - jw) < 2))
        fb[i] = np.where(forced, 1000.0 + jw, np.where(valid, 0.0, -1000.0 - jw))
    t["fb"] = fb
    a = np.arange(NCMP)
    cval = ((a - NPADR * 128 + 128 * core) >= 0).astype(np.float32)
    t["cval"] = np.ascontiguousarray(cval.reshape(7, 128).T)
    t["ident"] = np.eye(128, dtype=np.float32).astype(ml_dtypes.bfloat16)
    return t


TABLE_SPECS = {"dqk": ([3, 128, 128], F32), "dqk4": ([4, 128, 128], F32), "lm4": ([4, 128, 128], F32), "lm": ([6, 128, 128], F32), "cd": ([2, 16, 128, 128], F32),
               "ov": ([128, 33], BF16), "wt": ([128, 8192], BF16), "fb": ([16, 128, 256], F32),
               "cval": ([128, 7], F32), "ident": ([128, 128], BF16)}


def emit_p2(nc, S, I, O, pfx="p2", do_a=True, do_b=True, b_groups=(0, 1), b_qblocks=tuple(range(16)), b_stage=9, b_sub=9, a_heads=tuple(range(16))):
    B = lambda n: pfx + n
    with contextlib.ExitStack() as es:
        sb = lambda name, shape, dt: es.enter_context(nc.sbuf_tensor(pfx + name, shape, dt))
        ps = lambda name, shape, dt: es.enter_context(nc.psum_tensor(pfx + name, shape, dt))
        dqk = sb("dqk", [128, 3, 128], F32)
        idb = sb("idb", [128, 128], BF16)
        NTP = 6
        LAG = 4
        tmpt = [sb(f"tmp{i}", [128, 512], F32) for i in range(NTP)]
        ptt = [sb(f"pt{i}", [128, 512], BF16) for i in range(NTP)]
        dqk4 = sb("dqk4", [128, 4, 128], F32)
        pend = []

        def flush():
            while pend:
                pend.pop(0)[1]()
        rd = [sb(f"rd{i}", [128, 4], F32) for i in range(2)]
        obank = [ps(f"ob{i}", [128, 512], F32) for i in range(2)]
        NST = 5
        stb = [ps(f"st{i}", [128, 512], F32) for i in range(NST)]
        xbank = obank
        selTp = ps("selTp", [128, 2, 128], BF16)
        with nc.allow_non_contiguous_dma(reason="small tables"):
            S.dma("sp", dqk[:], I["dqk"].rearrange("t k q -> k t q"), reads=[], writes=[B("dqk")], tag=B("dqk"))
        S.dma("sp", idb[:], I["ident"][:, :], reads=[], writes=[B("idb")], tag=B("idb"))
        with nc.allow_non_contiguous_dma(reason="small tables"):
            S.dma("sp", dqk4[:], I["dqk4"].rearrange("t k q -> k t q"), reads=[], writes=[B("dqk4")], tag=B("dqk4"))
        cnt = {"st": 0, "tp": 0, "ob": 0, "rd": 0, "otmp": 0, "osta": 0}

        def unitN(blocks, q_ap, oacc, oname, first, last, rd_list, tab_fn, cbias, ncols=129, extra_fn=None):
            nbk = len(blocks)
            w = nbk * 128
            si = cnt["st"] % NST
            cnt["st"] += 1
            stn = B(f"st{si}")
            for b, (kT_ap, v_ap, mask, extra) in enumerate(blocks):
                st_b = stb[si][:, b * 128:(b + 1) * 128]
                S.op("pe", lambda: nc.tensor.matmul(st_b, kT_ap, q_ap, start=True, stop=(mask is None)),
                     reads=rd_list, writes=[stn])
                if mask is not None:
                    S.op("pe", lambda: nc.tensor.matmul(st_b, mask[0], mask[1], start=False, stop=True),
                         reads=mask[2], writes=[stn])
            ti = cnt["tp"] % NTP
            cnt["tp"] += 1
            tab_fn(tmpt[ti][:, 0:w], stb[si][:, 0:w], stn, B(f"tmp{ti}"))
            S.op("act", lambda: nc.scalar.activation(ptt[ti][:, 0:w], tmpt[ti][:, 0:w], AF.Exp, bias=float(cbias), scale=1.0),
                 reads=[B(f"tmp{ti}")], writes=[B(f"pt{ti}")])

            def back(ti=ti, oacc=oacc, oname=oname, first=first, last=last, blocks=blocks, ncols=ncols, rd_list=rd_list):
                for b, (kT_ap, v_ap, mask, extra) in enumerate(blocks):
                    lb = last and b == len(blocks) - 1
                    p_b = ptt[ti][:, b * 128:(b + 1) * 128]
                    S.op("pe", lambda: nc.tensor.matmul(oacc[:, 0:ncols], p_b, v_ap, start=(first and b == 0), stop=lb,
                                                        skip_group_check=(extra is not None)),
                         reads=[B(f"pt{ti}")] + rd_list, writes=[oname])
                    if extra is not None:
                        S.op("pe", lambda: nc.tensor.matmul(extra[0], p_b, extra[1], start=False, stop=lb, skip_group_check=True),
                             reads=[B(f"pt{ti}")] + extra[2], writes=[oname])

            pend.append(("u", back))
            while sum(1 for k, _ in pend if k == "u") > LAG:
                pend.pop(0)[1]()

        def unit(kT_ap, q_ap, v_ap, oacc, oname, first, last, rd_list, tab_fn, cbias, ncols=129, mask=None, extra=None):
            unitN([(kT_ap, v_ap, mask, extra)], q_ap, oacc, oname, first, last, rd_list, tab_fn, cbias, ncols=ncols)

        def next_ob():
            oi = cnt["ob"] % 2
            cnt["ob"] += 1
            return obank[oi], B(f"ob{oi}")

        if do_a:
            with contextlib.ExitStack() as ea:
                sa = lambda name, shape, dt: ea.enter_context(nc.sbuf_tensor(pfx + name, shape, dt))
                lm = sa("lm", [128, 6, 128], F32)
                kTa = [sa(f"kTa{i}", [128, 2, TOK], BF16) for i in range(2)]
                vA = [sa(f"vA{i}", [128, 32, 129], BF16) for i in range(2)]
                qA = [sa(f"qA{i}", [128, TOK], BF16) for i in range(2)]
                ta = [sa(f"ta{i}", [128, 6, 128], F32) for i in range(2)]
                lm4 = sa("lm4", [128, 4, 128], F32)
                ta4 = [sa(f"ta4{i}", [128, 4, 128], F32) for i in range(2)]
                ost = [sa(f"osta{i}", [128, 128], F32) for i in range(3)]
                with nc.allow_non_contiguous_dma(reason="small tables"):
                    S.dma("sp", lm[:], I["lm"].rearrange("t k q -> k t q"), reads=[], writes=[B("lm")], tag=B("lm"))
                    S.dma("sp", lm4[:], I["lm4"].rearrange("t k q -> k t q"), reads=[], writes=[B("lm4")], tag=B("lm4"))
                no = 0
                for h in a_heads:
                    hb = h % 2
                    sl = SLOPES[h]
                    S.dma("sp", kTa[hb][:], I["kaTw"][:, h, :, :].rearrange("r d t -> d r t"),
                          reads=[], writes=[B(f"kTa{hb}")], tag=B(f"kTa{hb}"))
                    for r in range(2):
                        S.dma("sp", vA[hb][:, r * 16:(r + 1) * 16, :], I["vaw"][r, :, h, :].rearrange("(b p) e -> p b e", p=128),
                              reads=[], writes=[B(f"vA{hb}")], tag=B(f"vA{hb}"))
                    S.dma("sp", qA[hb][:], I["qaT"][h, :, :], reads=[], writes=[B(f"qA{hb}")], tag=B(f"qA{hb}"))
                    for ty in range(6):
                        S.op("dve", lambda: nc.vector.scalar_tensor_tensor(ta[hb][:, ty, :], dqk[:, 0, :], -sl, lm[:, ty, :], ALU.mult, ALU.add),
                             reads=[B("dqk"), B("lm")], writes=[B(f"ta{hb}")])
                    S.op("dve", lambda: nc.vector.scalar_tensor_tensor(ta4[hb][:].rearrange("k b q -> k (b q)"), dqk4[:].rearrange("k b q -> k (b q)"), -sl,
                                                                       lm4[:].rearrange("k b q -> k (b q)"), ALU.mult, ALU.add),
                         reads=[B("dqk4"), B("lm4")], writes=[B(f"ta4{hb}")])
                    kflat = kTa[hb][:].rearrange("d r t -> d (r t)")
                    rdl = [B(f"kTa{hb}"), B(f"qA{hb}"), B(f"vA{hb}")]
                    for i in range(16):
                        oacc, on = next_ob()
                        q_ap = qA[hb][:, i * 128:(i + 1) * 128]
                        for off in range(5):
                            wbk = 16 + i - off
                            ty = a_type(off)

                            def tab(tmp, st, stn, tmpn, ty=ty):
                                S.op("dve", lambda: nc.vector.tensor_tensor(tmp, st, ta[hb][:, ty, :], ALU.add),
                                     reads=[stn, B(f"ta{hb}")], writes=[tmpn])

                            unit(kflat[:, wbk * 128:(wbk + 1) * 128], q_ap, vA[hb][:, wbk, :],
                                 oacc, on, off == 0, False, rdl, tab, -sl * 128 * off)
                        for off0 in (5, 9, 13):
                            offs = [o_ for o_ in range(off0, off0 + 4) if o_ < 16]
                            blocks = [(kflat[:, (16 + i - o_) * 128:(16 + i - o_ + 1) * 128], vA[hb][:, 16 + i - o_, :], None, None) for o_ in offs]

                            def tab4(tmp, st, stn, tmpn, w=len(offs) * 128):
                                S.op("dve", lambda: nc.vector.tensor_tensor(tmp, st, ta4[hb][:].rearrange("k b q -> k (b q)")[:, 0:w], ALU.add),
                                     reads=[stn, B(f"ta4{hb}")], writes=[tmpn])

                            unitN(blocks, q_ap, oacc, on, False, False, rdl, tab4, -sl * 128 * off0)

                        def tab16(tmp, st, stn, tmpn):
                            S.op("dve", lambda: nc.vector.tensor_tensor(tmp, st, ta[hb][:, 5, :], ALU.add),
                                 reads=[stn, B(f"ta{hb}")], writes=[tmpn])

                        unit(kflat[:, i * 128:(i + 1) * 128], q_ap, vA[hb][:, i, :], oacc, on, False, True, rdl, tab16, -sl * 128 * 16)
                        def a_fin(oacc=oacc, on=on, h=h, i=i):
                            ri = cnt["rd"] % 2
                            cnt["rd"] += 1
                            S.op("dve", lambda: nc.vector.reciprocal(rd[ri][:, 0:1], oacc[:, 128:129]), reads=[on], writes=[B(f"rd{ri}")])
                            oi = cnt["osta"] % 3
                            cnt["osta"] += 1
                            S.op("act", lambda: nc.scalar.activation(ost[oi][:], oacc[:, 0:128], AF.Copy, scale=rd[ri][:, 0:1]),
                                 reads=[on, B(f"rd{ri}")], writes=[B(f"osta{oi}")])
                            S.dma("sp", O["oa"][i * 128:(i + 1) * 128, h * 128:(h + 1) * 128], ost[oi][:],
                                  reads=[B(f"osta{oi}")], writes=[], tag=B(f"osta{oi}"))

                        pend.append(("f", a_fin))
                flush()

        S.drain()
        if do_b:
            with contextlib.ExitStack() as eb:
                sB = lambda name, shape, dt: eb.enter_context(nc.sbuf_tensor(pfx + name, shape, dt))
                cd = sB("cd", [128, 32, 128], F32)
                fbt = [sB(f"fb{i}", [128, 256], F32) for i in range(2)]
                ov = sB("ov", [128, 33], BF16)
                wt = sB("wt", [128, 8192], BF16)
                cval = sB("cval", [128, 7], F32)
                ksT = sB("ksT", [128, 7, TOK], BF16)
                vs = sB("vs", [128, 112, 129], BF16)
                kwT = sB("kwT", [128, 20 * 128], BF16)
                vw = sB("vw", [128, 20, 129], BF16)
                kcmpT = sB("kcmpT", [128, NCMP], BF16)
                vcmp = sB("vcmp", [128, 7, 129], BF16)
                qB = [sB(f"qB{i}", [128, 8, 128], BF16) for i in range(2)]
                glt = [sB(f"gl{i}", [128, 48], F32) for i in range(2)]
                gs = sB("gs", [128, 48], F32)
                impa = sB("impa", [128, 256], F32)
                work = sB("work", [128, 256], F32)
                m8 = sB("m8", [128, 16], F32)
                sel = sB("sel", [128, 256], F32)
                selb = sB("selb", [128, 256], BF16)
                selT = sB("selT", [128, 2, 128], BF16)
                obst = [sB(f"obst{i}", [128, 8, 128], F32) for i in range(2)]
                otmp = [sB(f"otmp{i}", [128, 128], F32) for i in range(2)]
                with nc.allow_non_contiguous_dma(reason="small tables"):
                    S.dma("sp", cd[:], I["cd"].rearrange("a i k q -> k (a i) q"), reads=[], writes=[B("cd")], tag=B("cd"))
                S.dma("sp", ov[:], I["ov"][:, :], reads=[], writes=[B("ov")], tag=B("ov"))
                S.dma("sp", wt[:], I["wt"][:, :], reads=[], writes=[B("wt")], tag=B("wt"))
                S.dma("sp", cval[:], I["cval"][:, :], reads=[], writes=[B("cval")], tag=B("cval"))
                ncoef = 0
                nobst = 0
                for g in b_groups:
                    S.dma("sp", ksT[:], I["kvTw"][:, 2, g, :, :].rearrange("r d t -> d r t"), reads=[], writes=[B("ksT")], tag=B("ksT"))
                    for r in range(7):
                        S.dma("sp", vs[:, r * 16:(r + 1) * 16, :], I["vsww"][r, 0, :, g, :].rearrange("(b p) e -> p b e", p=128),
                              reads=[], writes=[B("vs")], tag=B("vs"))
                    S.dma("sp", kwT[:, 0:512], I["kvTw"][5, 3, g, :, TOK - 512:TOK], reads=[], writes=[B("kwT")], tag=B("kwT"))
                    S.dma("sp", kwT[:, 512:], I["kvTw"][6, 3, g, :, :], reads=[], writes=[B("kwT")], tag=B("kwT"))
                    S.dma("sp", vw[:, 0:4, :], I["vsww"][5, 1, TOK - 512:TOK, g, :].rearrange("(b p) e -> p b e", p=128),
                          reads=[], writes=[B("vw")], tag=B("vw"))
                    S.dma("sp", vw[:, 4:20, :], I["vsww"][6, 1, :, g, :].rearrange("(b p) e -> p b e", p=128),
                          reads=[], writes=[B("vw")], tag=B("vw"))
                    if b_stage < 1:
                        continue
                    with contextlib.ExitStack() as ec:
                        sc = lambda name, shape, dt: ec.enter_context(nc.sbuf_tensor(pfx + name + str(g), shape, dt))
                        tT = sc("tT", [128, 7, TOK], BF16)
                        w1 = sc("w1", [128, 32, 256], BF16)
                        w2 = sc("w2", [128, 2, 128], BF16)
                        peT = sc("peT", [128, 32], BF16)
                        pesb = sc("pesb", [32, 128], BF16)
                        b1 = sc("b1", [128, 2], F32)
                        u = sc("u", [128, 448], F32)
                        u2 = sc("u2", [128, 448], F32)
                        hb_ = sc("hb", [128, 2, NCMP], BF16)
                        S.op("pool", lambda: nc.gpsimd.memset(hb_[:], 0.0), writes=[B("hb")])
                        for which in range(2):
                            S.dma("sp", tT[:], I["kvTw"][:, which, g, :, :].rearrange("r d t -> d r t"), reads=[], writes=[B("tT")], tag=B("tT"))
                            S.dma("pool", w1[:], I["cw1"][which].rearrange("(l p) n -> p l n", p=128), reads=[], writes=[B("w1")], tag=B("w1"))
                            S.dma("pool", w2[:], I["cw2"][which].rearrange("(c p) n -> p c n", p=128), reads=[], writes=[B("w2")], tag=B("w2"))
                            S.dma("pool", pesb[:], I["cpe"][which], reads=[], writes=[B("pesb")], tag=B("pesb"))
                            S.op("pe", lambda: nc.tensor.transpose(selTp[:, 0, 0:32], pesb[:, :], idb[0:32, 0:32]),
                                 reads=[B("pesb"), B("idb")], writes=[B("selTp")])
                            S.op("dve", lambda: nc.vector.tensor_copy(peT[:], selTp[:, 0, 0:32]), reads=[B("selTp")], writes=[B("peT")])
                            tflat = tT[:].rearrange("d r t -> d (r t)")
                            for hc in range(2):
                                xb_ = xbank[0]
                                for l in range(32):
                                    S.op("pe", lambda: nc.tensor.matmul(xb_[:, 0:1], w1[:, l, hc * 128:(hc + 1) * 128], peT[:, l:l + 1],
                                                                        start=(l == 0), stop=(l == 31)),
                                         reads=[B("w1"), B("peT")], writes=[B("ob0")])
                                S.op("dve", lambda: nc.vector.tensor_copy(b1[:, hc:hc + 1], xb_[:, 0:1]), reads=[B("ob0")], writes=[B("b1")])
                            for cb in range(2):
                                c0 = cb * 448
                                ncol = 448 if cb == 0 else 447
                                for hc in range(2):
                                    xb_ = xbank[1]
                                    for l in range(32):
                                        rhs = tflat.rearrange("d (c s) -> d c s", s=16)[:, c0 + l // 16:c0 + l // 16 + ncol, l % 16]
                                        S.op("pe", lambda: nc.tensor.matmul(xb_[:, 0:ncol], w1[:, l, hc * 128:(hc + 1) * 128], rhs,
                                                                            start=(l == 0), stop=(l == 31)),
                                             reads=[B("w1"), B("tT")], writes=[B("ob1")])
                                    S.op("dve", lambda: nc.vector.tensor_scalar(u[:, 0:ncol], xb_[:, 0:ncol], b1[:, hc:hc + 1], None, ALU.add),
                                         reads=[B("ob1"), B("b1")], writes=[B("u")])
                                    S.op("dve", lambda: nc.vector.tensor_tensor(u2[:, 0:ncol], u[:, 0:ncol], u[:, 0:ncol], ALU.mult),
                                         reads=[B("u")], writes=[B("u2")])
                                    S.op("dve", lambda: nc.vector.tensor_scalar(u2[:, 0:ncol], u2[:, 0:ncol], 0.044715, 1.0, ALU.mult, ALU.add),
                                         reads=[B("u2")], writes=[B("u2")])
                                    S.op("dve", lambda: nc.vector.tensor_tensor(u2[:, 0:ncol], u2[:, 0:ncol], u[:, 0:ncol], ALU.mult),
                                         reads=[B("u2"), B("u")], writes=[B("u2")])
                                    S.op("act", lambda: nc.scalar.activation(u2[:, 0:ncol], u2[:, 0:ncol], AF.Sigmoid, scale=1.5957691216057308),
                                         reads=[B("u2")], writes=[B("u2")])
                                    S.op("dve", lambda: nc.vector.tensor_tensor(hb_[:, hc, c0:c0 + ncol], u2[:, 0:ncol], u[:, 0:ncol], ALU.mult),
                                         reads=[B("u2"), B("u")], writes=[B("hb")])
                            if which == 0:
                                for cb in range(2):
                                    xb_ = xbank[0]
                                    for hc in range(2):
                                        S.op("pe", lambda: nc.tensor.matmul(xb_[:, 0:448], w2[:, hc, :], hb_[:, hc, cb * 448:(cb + 1) * 448],
                                                                            start=(hc == 0), stop=(hc == 1)),
                                             reads=[B("w2"), B("hb")], writes=[B("ob0")])
                                    S.op("dve", lambda: nc.vector.tensor_copy(kcmpT[:, cb * 448:(cb + 1) * 448], xb_[:, 0:448]),
                                         reads=[B("ob0")], writes=[B("kcmpT")])
                            else:
                                for bb in range(7):
                                    xb_ = xbank[bb % 2]
                                    xn = B(f"ob{bb % 2}")
                                    for hc in range(2):
                                        S.op("pe", lambda: nc.tensor.matmul(xb_[:, 0:128], hb_[:, hc, bb * 128:(bb + 1) * 128], w2[:, hc, :],
                                                                            start=(hc == 0), stop=(hc == 1)),
                                             reads=[B("w2"), B("hb")], writes=[xn])
                                    S.op("dve", lambda: nc.vector.tensor_scalar_mul(vcmp[:, bb, 0:128], xb_[:, 0:128], cval[:, bb:bb + 1]),
                                         reads=[xn, B("cval")], writes=[B("vcmp")])
                                    S.op("dve", lambda: nc.vector.tensor_copy(vcmp[:, bb, 128:129], cval[:, bb:bb + 1]),
                                         reads=[B("cval")], writes=[B("vcmp")])
                    S.drain()
                    if b_stage < 2:
                        continue
                    ksflat = ksT[:].rearrange("d r t -> d (r t)")
                    for i in b_qblocks:
                        qi = (g * 16 + i) % 2
                        qn = B(f"qB{qi}")
                        S.dma("sp", qB[qi][:], I["qbT"][g * 8:(g + 1) * 8, :, i * 128:(i + 1) * 128].rearrange("h d t -> d h t"),
                              reads=[], writes=[qn], tag=qn)
                        S.dma("sp", glt[qi][:], I["gl"][i * 128:(i + 1) * 128, :], reads=[], writes=[B(f"gl{qi}")], tag=B(f"gl{qi}"))
                        S.op("act", lambda: nc.scalar.activation(gs[:], glt[qi][:], AF.Sigmoid), reads=[B(f"gl{qi}")], writes=[B("gs")])
                        S.dma("sp", fbt[qi][:], I["fb"][i], reads=[], writes=[B(f"fb{qi}")], tag=B(f"fb{qi}"))
                        S.op("dve", lambda: nc.vector.tensor_copy(impa[:], fbt[qi][:]), reads=[B(f"fb{qi}")], writes=[B("impa")])
                        obi = nobst % 2
                        nobst += 1
                        ob_t = obst[obi]
                        obn = B(f"obst{obi}")

                        def finish_head(oacc, on, hl, br, first_branch, imp_lo=None):
                            h = g * 8 + hl
                            ri = cnt["rd"] % 2
                            cnt["rd"] += 1
                            rdn = B(f"rd{ri}")
                            S.op("dve", lambda: nc.vector.tensor_scalar(rd[ri][:, 2:3], oacc[:, 128:129], 1e-30, None, ALU.max),
                                 reads=[on], writes=[rdn])
                            S.op("dve", lambda: nc.vector.reciprocal(rd[ri][:, 0:1], rd[ri][:, 2:3]), reads=[rdn], writes=[rdn])
                            S.op("dve", lambda: nc.vector.scalar_tensor_tensor(rd[ri][:, 0:1], rd[ri][:, 2:3], 1e-20, rd[ri][:, 0:1], ALU.is_gt, ALU.mult),
                                 reads=[rdn], writes=[rdn])
                            S.op("dve", lambda: nc.vector.tensor_tensor(rd[ri][:, 1:2], rd[ri][:, 0:1], gs[:, 3 * h + br:3 * h + br + 1], ALU.mult),
                                 reads=[rdn, B("gs")], writes=[rdn])
                            if imp_lo is not None:
                                S.op("act", lambda: nc.scalar.activation(work[:, imp_lo:225], oacc[:, 132 + imp_lo:132 + 225], AF.Copy, scale=rd[ri][:, 0:1]),
                                     reads=[on, rdn], writes=[B("work")])
                                S.op("pool", lambda: nc.gpsimd.tensor_tensor(impa[:, imp_lo:225], impa[:, imp_lo:225], work[:, imp_lo:225], ALU.add),
                                     reads=[B("work"), B("impa")], writes=[B("impa")])
                            if first_branch:
                                S.op("act", lambda: nc.scalar.activation(ob_t[:, hl, :], oacc[:, 0:128], AF.Copy, scale=rd[ri][:, 1:2]),
                                     reads=[on, rdn], writes=[obn])
                            else:
                                oi_ = cnt["otmp"] % 2
                                cnt["otmp"] += 1
                                S.op("act", lambda: nc.scalar.activation(otmp[oi_][:], oacc[:, 0:128], AF.Copy, scale=rd[ri][:, 1:2]),
                                     reads=[on, rdn], writes=[B(f"otmp{oi_}")])
                                S.op("pool", lambda: nc.gpsimd.tensor_tensor(ob_t[:, hl, :], ob_t[:, hl, :], otmp[oi_][:], ALU.add),
                                     reads=[B(f"otmp{oi_}"), obn], writes=[obn])

                        for hl in range(8 if b_sub >= 1 else 0):
                            h = g * 8 + hl
                            sl = SLOPES[h]
                            oacc, on = next_ob()
                            nb = NBC[h]
                            for bo in range(nb):
                                bblk = 6 - bo
                                tix = (0 if bo == 0 else 1) * 16 + i

                                def tab(tmp, st, stn, tmpn, tix=tix, sl=sl):
                                    S.op("dve", lambda: nc.vector.scalar_tensor_tensor(tmp, cd[:, tix, :], -sl, st, ALU.mult, ALU.add),
                                         reads=[stn, B("cd")], writes=[tmpn])

                                unit(kcmpT[:, bblk * 128:(bblk + 1) * 128], qB[qi][:, hl, :], vcmp[:, bblk, :], oacc, on,
                                     bo == 0, bo == nb - 1, [B("kcmpT"), qn, B("vcmp")], tab,
                                     (0.0 if bo == 0 else -sl * 2048 * (bo - 1)),
                                     extra=((oacc[:, 132 + 32 * bblk:132 + 32 * bblk + 33], ov[:], [B("ov")]) if b_sub >= 2 else None))
                            pend.append(("f", functools.partial(finish_head, oacc, on, hl, 0, True, imp_lo=32 * (7 - nb))))
                        flush()
                        if b_stage < 3:
                            continue
                        S.op("dve", lambda: nc.vector.max(m8[:, 0:8], impa[:]), reads=[B("impa")], writes=[B("m8")])
                        S.op("dve", lambda: nc.vector.match_replace(work[:], m8[:, 0:8], impa[:], -1.0e30),
                             reads=[B("impa"), B("m8")], writes=[B("work")])
                        S.op("dve", lambda: nc.vector.max(m8[:, 8:16], work[:]), reads=[B("work")], writes=[B("m8")])
                        S.op("dve", lambda: nc.vector.tensor_scalar(sel[:], impa[:], m8[:, 15:16], None, ALU.is_ge),
                             reads=[B("impa"), B("m8")], writes=[B("sel")])
                        S.op("dve", lambda: nc.vector.scalar_tensor_tensor(sel[:], impa[:], -500.0, sel[:], ALU.is_gt, ALU.mult),
                             reads=[B("impa"), B("sel")], writes=[B("sel")])
                        S.op("dve", lambda: nc.vector.tensor_scalar(selb[:], sel[:], -1.0, None, ALU.add),
                             reads=[B("sel")], writes=[B("selb")])
                        for c2 in range(2):
                            S.op("pe", lambda: nc.tensor.transpose(selTp[:, c2, :], selb[:, c2 * 128:(c2 + 1) * 128], idb[:]),
                                 reads=[B("selb"), B("idb")], writes=[B("selTp")])
                        S.op("act", lambda: nc.scalar.copy(selT[:], selTp[:]), reads=[B("selTp")], writes=[B("selT")])
                        if b_stage < 4:
                            continue
                        for hl in range(8):
                            h = g * 8 + hl
                            sl = SLOPES[h]
                            oacc, on = next_ob()
                            nb = NBS[h]
                            rdl = [B("ksT"), qn, B("vs")]

                            def blk(off):
                                wbk = 96 + i - off
                                return (ksflat[:, wbk * 128:(wbk + 1) * 128], vs[:, wbk, :],
                                        (wt[:, 128 * (wbk % 64):128 * (wbk % 64) + 128], selT[:, wbk // 64, :], [B("wt"), B("selT")]), None)

                            def tab0(tmp, st, stn, tmpn, sl=sl):
                                S.op("dve", lambda: nc.vector.scalar_tensor_tensor(tmp, dqk[:, 1, :], -sl, st, ALU.mult, ALU.add),
                                     reads=[stn, B("dqk")], writes=[tmpn])

                            unitN([blk(0)], qB[qi][:, hl, :], oacc, on, True, nb == 1, rdl, tab0, 0.0)
                            for off0 in range(1, nb, 4):
                                offs = list(range(off0, min(off0 + 4, nb)))

                                def tab4(tmp, st, stn, tmpn, sl=sl, w=len(offs) * 128):
                                    S.op("dve", lambda: nc.vector.scalar_tensor_tensor(tmp, dqk4[:].rearrange("k b q -> k (b q)")[:, 0:w], -sl, st, ALU.mult, ALU.add),
                                         reads=[stn, B("dqk4")], writes=[tmpn])

                                unitN([blk(o_) for o_ in offs], qB[qi][:, hl, :], oacc, on, False, offs[-1] == nb - 1, rdl, tab4, -sl * 128 * off0)
                            pend.append(("f", functools.partial(finish_head, oacc, on, hl, 1, False)))
                        if b_stage < 5:
                            continue
                        for hl in range(8):
                            h = g * 8 + hl
                            sl = SLOPES[h]
                            oacc, on = next_ob()
                            for off in range(5):
                                wbk = 4 + i - off
                                ty = 1 if off == 0 else 2 if off == 4 else 0

                                def tab(tmp, st, stn, tmpn, ty=ty, sl=sl):
                                    S.op("dve", lambda: nc.vector.scalar_tensor_tensor(tmp, dqk[:, ty, :], -sl, st, ALU.mult, ALU.add),
                                         reads=[stn, B("dqk")], writes=[tmpn])

                                unit(kwT[:, wbk * 128:(wbk + 1) * 128], qB[qi][:, hl, :], vw[:, wbk, :], oacc, on,
                                     off == 0, off == 4, [B("kwT"), qn, B("vw")], tab, -sl * 128 * off)
                            pend.append(("f", functools.partial(finish_head, oacc, on, hl, 2, False)))
                        flush()
                        S.dma("sp", O["ob"][i * 128:(i + 1) * 128, g * 1024:(g + 1) * 1024], ob_t[:].rearrange("q h d -> q (h d)"),
                              reads=[obn], writes=[], tag=obn)


def emit_p3(nc, S, x, oa, ob, z, goa, gob, wout, ident, xnew, pfx="p3"):
    B = lambda n: pfx + n
    TP = 1024
    NT = TP // 128
    KC = D // 128
    with contextlib.ExitStack() as es:
        sb = lambda name, shape, dt: es.enter_context(nc.sbuf_tensor(pfx + name, shape, dt))
        yT = sb("yT", [128, KC, TP], BF16)
        wb = [sb(f"wb{i}", [128, KC, 512], BF16) for i in range(2)]
        gO = sb("gO", [128, D], F32)
        ot = [sb(f"ot{i}", [128, 2048], F32) for i in range(2)]
        zt = [sb(f"zt{i}", [128, 2048], F32) for i in range(2)]
        yb = sb("yb", [128, 2048], BF16)
        idb = sb("idb", [128, 128], BF16)
        ss = sb("ss", [128, 2], F32)
        xr = [sb(f"xr{i}", [128, 512], F32) for i in range(3)]
        xo = [sb(f"xo{i}", [128, 512], F32) for i in range(3)]
        acc = [es.enter_context(nc.psum_tensor(pfx + f"acc{i}", [128, 512], F32)) for i in range(6)]
        tps = [es.enter_context(nc.psum_tensor(pfx + f"tp{i}", [128, 512], BF16)) for i in range(2)]
        S.dma("sp", idb[:], ident[:, :], reads=[], writes=[B("idb")], tag=B("idb"))
        S.dma("sp", gO[:, 0:2048], goa.partition_broadcast(128), reads=[], writes=[B("gO")], tag=B("gO"))
        S.dma("sp", gO[:, 2048:4096], gob.partition_broadcast(128), reads=[], writes=[B("gO")], tag=B("gO"))
        cnt = {"acc": 0, "tp": 0, "w": 0, "x": 0, "ld": 0}
        for p in range(TOK // TP):
            t0 = p * TP
            for tt in range(NT):
                tok0 = t0 + tt * 128
                for half, osrc in ((0, oa), (1, ob)):
                    li = cnt["ld"] % 2
                    cnt["ld"] += 1
                    o_t, z_t = ot[li], zt[li]
                    on, zn = B(f"ot{li}"), B(f"zt{li}")
                    S.dma("sp", o_t[:], osrc[tok0:tok0 + 128, :], reads=[], writes=[on], tag=on)
                    S.dma("sp", z_t[:], z[tok0:tok0 + 128, half * 2048:(half + 1) * 2048], reads=[], writes=[zn], tag=zn)
                    S.op("act", lambda: nc.scalar.activation(yb[:], o_t[:], AF.Square, accum_out=ss[:, 0:1]),
                         reads=[on], writes=[B("yb"), B("ss")])
                    S.op("dve", lambda: nc.vector.tensor_scalar(ss[:, 1:2], ss[:, 0:1], 1.0 / 2048, EPS, ALU.mult, ALU.add),
                         reads=[B("ss")], writes=[B("ss")])
                    S.op("act", lambda: nc.scalar.activation(ss[:, 1:2], ss[:, 1:2], AF.Sqrt), reads=[B("ss")], writes=[B("ss")])
                    S.op("dve", lambda: nc.vector.reciprocal(ss[:, 1:2], ss[:, 1:2]), reads=[B("ss")], writes=[B("ss")])
                    S.op("act", lambda: nc.scalar.activation(z_t[:], z_t[:], AF.Silu), reads=[zn], writes=[zn])
                    S.op("pool", lambda: nc.gpsimd.tensor_tensor(z_t[:], z_t[:], gO[:, half * 2048:(half + 1) * 2048], ALU.mult),
                         reads=[zn, B("gO")], writes=[zn])
                    S.op("dve", lambda: nc.vector.tensor_scalar_mul(o_t[:], o_t[:], ss[:, 1:2]), reads=[on, B("ss")], writes=[on])
                    S.op("dve", lambda: nc.vector.tensor_tensor(yb[:], o_t[:], z_t[:], ALU.mult), reads=[on, zn], writes=[B("yb")])
                    for k4 in range(4):
                        tpi = cnt["tp"] % 2
                        cnt["tp"] += 1
                        tp = tps[tpi]
                        for j in range(4):
                            kc = k4 * 4 + j
                            S.op("pe", lambda: nc.tensor.transpose(tp[:, j * 128:(j + 1) * 128], yb[:, kc * 128:(kc + 1) * 128], idb[:]),
                                 reads=[B("yb"), B("idb")], writes=[B(f"tp{tpi}")])
                        dst = yT[:, half * 16 + k4 * 4:half * 16 + k4 * 4 + 4, tt * 128:(tt + 1) * 128]
                        src = tp[:].rearrange("p (j t) -> p j t", t=128)
                        if k4 % 2 == 0:
                            S.op("dve", lambda: nc.vector.tensor_copy(dst, src), reads=[B(f"tp{tpi}")], writes=[B("yT")])
                        else:
                            S.op("act", lambda: nc.scalar.copy(dst, src), reads=[B(f"tp{tpi}")], writes=[B("yT")])
            for ch in range(8):
                wi = cnt["w"] % 2
                cnt["w"] += 1
                w, wn = wb[wi], B(f"wb{wi}")
                S.dma("pool", w[:], wout[:, ch * 512:(ch + 1) * 512].rearrange("(kc p) n -> p kc n", p=128),
                      reads=[], writes=[wn], tag=wn)
                for tt in range(NT):
                    tok0 = t0 + tt * 128
                    ai = cnt["acc"] % 6
                    cnt["acc"] += 1
                    a, an = acc[ai], B(f"acc{ai}")
                    xi = cnt["x"] % 3
                    cnt["x"] += 1
                    xn = B(f"xr{xi}")
                    S.dma("sp", xr[xi][:], x[tok0:tok0 + 128, ch * 512:(ch + 1) * 512], reads=[], writes=[xn], tag=xn)
                    for kc in range(KC):
                        S.op("pe", lambda: nc.tensor.matmul(a[:], yT[:, kc, tt * 128:(tt + 1) * 128], w[:, kc, :],
                                                            start=(kc == 0), stop=(kc == KC - 1)),
                             reads=[wn, B("yT")], writes=[an])
                    xon = B(f"xo{xi}")
                    S.op("dve", lambda: nc.vector.tensor_tensor(xo[xi][:], a[:], xr[xi][:], ALU.add), reads=[an, xn], writes=[xon])
                    S.dma("sp", xnew[tok0:tok0 + 128, ch * 512:(ch + 1) * 512], xo[xi][:], reads=[xon], writes=[], tag=xon)


def emit_p4(nc, S, xin, fg, out, pfx="p4"):
    B = lambda n: pfx + n
    with contextlib.ExitStack() as es:
        sb = lambda name, shape, dt: es.enter_context(nc.sbuf_tensor(pfx + name, shape, dt))
        gF = sb("gF", [128, D], F32)
        xt = [sb(f"xt{i}", [128, D], F32) for i in range(2)]
        junk = sb("junk", [128, D], BF16)
        ss = sb("ss", [128, 2], F32)
        S.dma("sp", gF[:], fg.partition_broadcast(128), reads=[], writes=[B("gF")], tag=B("gF"))
        for tt in range(TOK // 128):
            xi = tt % 2
            xn = B(f"xt{xi}")
            S.dma("sp", xt[xi][:], xin[tt * 128:(tt + 1) * 128, :], reads=[], writes=[xn], tag=xn)
            S.op("act", lambda: nc.scalar.activation(junk[:], xt[xi][:], AF.Square, accum_out=ss[:, 0:1]),
                 reads=[xn], writes=[B("junk"), B("ss")])
            S.op("dve", lambda: nc.vector.tensor_scalar(ss[:, 1:2], ss[:, 0:1], 1.0 / D, EPS, ALU.mult, ALU.add),
                 reads=[B("ss")], writes=[B("ss")])
            S.op("act", lambda: nc.scalar.activation(ss[:, 1:2], ss[:, 1:2], AF.Sqrt), reads=[B("ss")], writes=[B("ss")])
            S.op("dve", lambda: nc.vector.reciprocal(ss[:, 1:2], ss[:, 1:2]), reads=[B("ss")], writes=[B("ss")])
            S.op("dve", lambda: nc.vector.tensor_scalar_mul(xt[xi][:], xt[xi][:], ss[:, 1:2]), reads=[xn, B("ss")], writes=[xn])
            S.op("dve", lambda: nc.vector.tensor_tensor(xt[xi][:], xt[xi][:], gF[:], ALU.mult), reads=[xn, B("gF")], writes=[xn])
            S.dma("sp", out[tt * 128:(tt + 1) * 128, :], xt[xi][:], reads=[xn], writes=[], tag=xn)


P2_IN_SPECS = {
    "qaT": ([NHA, 128, TOK], BF16), "qbT": ([NHB, 128, TOK], BF16), "gl": ([TOK, 48], F32),
    "kaTw": ([2, NHA, 128, TOK], BF16), "vaw": ([2, TOK, NHA, 129], BF16),
    "kvTw": ([7, 4, 2, 128, TOK], BF16), "vsww": ([7, 2, TOK, 2, 129], BF16),
    "cw1": ([2, 4096, 256], F32), "cw2": ([2, 256, 128], F32), "cpe": ([2, 32, 128], F32),
}


def window(arr, core, nback):
    out = np.zeros((nback + 1,) + arr.shape[1:], arr.dtype)
    for j in range(nback + 1):
        r = core - nback + j
        if r >= 0:
            out[j] = arr[r]
    return out


def p2_inputs(core, own, G, cw, tables):
    m = {"qaT": own["qaT"], "qbT": own["qbT"], "gl": own["gl"],
         "kaTw": window(G["kaT"], core, 1), "vaw": window(G["va"], core, 1),
         "kvTw": window(G["kvT"], core, NPADR), "vsww": window(G["vsw"], core, NPADR)}
    m.update(cw)
    m.update(tables)
    return m


def build_p2_only(do_a=True, do_b=True, **kw):
    nc = bass.Bass("TRN2", target_bir_lowering=False)
    I = {k: nc.dram_tensor(k, shp, dt, kind="ExternalInput").ap() for k, (shp, dt) in {**P2_IN_SPECS, **TABLE_SPECS}.items()}
    O = {k: nc.dram_tensor(k, [TOK, 2048], F32, kind="ExternalOutput").ap() for k in ("oa", "ob")}
    S = Sched(nc)
    emit_p2(nc, S, I, O, do_a=do_a, do_b=do_b, **kw)
    S.drain()
    return nc


def _dram_in(nc, specs):
    return {k: nc.dram_tensor(k, shp, dt, kind="ExternalInput").ap() for k, (shp, dt) in specs.items()}


POST_IN_SPECS = {"x": ([TOK, D], F32), "z": ([TOK, D], F32), "goa": ([2048], F32), "gob": ([2048], F32),
                 "wout": ([D, D], F32)}


def build_first():
    return build_p1_only()


def build_mid():
    nc = bass.Bass("TRN2", target_bir_lowering=False)
    I = _dram_in(nc, {**P2_IN_SPECS, **TABLE_SPECS, **POST_IN_SPECS, "ng": ([D], F32), "win": ([D, DIN], F32)})
    o = {k: nc.dram_tensor("n_" + k, shp, dt, kind="ExternalOutput").ap() for k, (shp, dt) in p1_out_specs().items()}
    xnew = nc.dram_tensor("xnew", [TOK, D], F32, kind="ExternalOutput").ap()
    O = {k: nc.dram_tensor("s_" + k, [TOK, 2048], F32).ap() for k in ("oa", "ob")}
    S = Sched(nc)
    emit_p2(nc, S, I, O)
    S.drain()
    emit_p3(nc, S, I["x"], O["oa"], O["ob"], I["z"], I["goa"], I["gob"], I["wout"], I["ident"], xnew)
    S.drain()
    emit_p1(nc, S, xnew, I["ng"], I["win"], I["ident"], o)
    S.drain()
    return nc


def build_last():
    nc = bass.Bass("TRN2", target_bir_lowering=False)
    I = _dram_in(nc, {**P2_IN_SPECS, **TABLE_SPECS, **POST_IN_SPECS, "fg": ([D], F32)})
    out = nc.dram_tensor("out", [TOK, D], F32, kind="ExternalOutput").ap()
    xnew = nc.dram_tensor("s_xnew", [TOK, D], F32).ap()
    O = {k: nc.dram_tensor("s_" + k, [TOK, 2048], F32).ap() for k in ("oa", "ob")}
    S = Sched(nc)
    emit_p2(nc, S, I, O)
    S.drain()
    emit_p3(nc, S, I["x"], O["oa"], O["ob"], I["z"], I["goa"], I["gob"], I["wout"], I["ident"], xnew)
    S.drain()
    emit_p4(nc, S, xnew, I["fg"], out)
    S.drain()
    return nc


def build_p3_test():
    nc = bass.Bass("TRN2", target_bir_lowering=False)
    I = _dram_in(nc, {**POST_IN_SPECS, "oa": ([TOK, 2048], F32), "ob": ([TOK, 2048], F32), "fg": ([D], F32),
                      "ident": ([128, 128], BF16)})
    out = nc.dram_tensor("out", [TOK, D], F32, kind="ExternalOutput").ap()
    xnew = nc.dram_tensor("xnew", [TOK, D], F32, kind="ExternalOutput").ap()
    S = Sched(nc)
    emit_p3(nc, S, I["x"], I["oa"], I["ob"], I["z"], I["goa"], I["gob"], I["wout"], I["ident"], xnew)
    S.drain()
    emit_p4(nc, S, xnew, I["fg"], out)
    S.drain()
    return nc


def _run(nc, in_maps):
    res = run_bass_kernel_spmd(nc, in_maps, core_ids=list(range(NCORES)))
    return res.results


def kernel(x, norm_g, w_in, cmp_k_pe, cmp_k_w1, cmp_k_w2, cmp_v_pe, cmp_v_w1, cmp_v_w2,
           out_g_a, out_g_b, w_out, final_g):
    f32 = lambda a: np.ascontiguousarray(np.asarray(a, dtype=np.float32))
    x2 = f32(x).reshape(SEQ, D)
    norm_g, w_in, w_out, final_g = f32(norm_g), f32(w_in), f32(w_out), f32(final_g)
    out_g_a, out_g_b = f32(out_g_a), f32(out_g_b)
    cw1 = np.stack([f32(cmp_k_w1), f32(cmp_v_w1)], 1)
    cw2 = np.stack([f32(cmp_k_w2), f32(cmp_v_w2)], 1)
    cpe = np.stack([f32(cmp_k_pe), f32(cmp_v_pe)], 1)
    tables = [make_tables(c) for c in range(NCORES)]
    ident = tables[0]["ident"]
    xcur = [x2[c * TOK:(c + 1) * TOK] for c in range(NCORES)]
    nc = build_first()
    res = _run(nc, [{"x": xcur[c], "ng": norm_g[0], "win": w_in[0], "ident": ident} for c in range(NCORES)])
    nc_mid = None
    out = None
    for l in range(DEPTH):
        G = {k: np.stack([np.asarray(res[c][k]) for c in range(NCORES)]) for k in ("kaT", "va", "kvT", "vsw")}
        cw = {"cw1": cw1[l], "cw2": cw2[l], "cpe": cpe[l]}
        in_maps = []
        for c in range(NCORES):
            m = p2_inputs(c, res[c], G, cw, tables[c])
            m.update(x=xcur[c], z=np.asarray(res[c]["z"]), goa=out_g_a[l], gob=out_g_b[l], wout=w_out[l])
            if l < DEPTH - 1:
                m.update(ng=norm_g[l + 1], win=w_in[l + 1])
            else:
                m.update(fg=final_g)
            in_maps.append(m)
        if l < DEPTH - 1:
            if nc_mid is None:
                nc_mid = build_mid()
            r = _run(nc_mid, in_maps)
            xcur = [np.asarray(r[c]["xnew"]) for c in range(NCORES)]
            res = [{k: np.asarray(r[c]["n_" + k]) for k in p1_out_specs()} for c in range(NCORES)]
        else:
            r = _run(build_last(), in_maps)
            out = np.concatenate([np.asarray(r[c]["out"]) for c in range(NCORES)], 0)
    return out.reshape(1, SEQ, D).astype(np.float32)
```
